# Optimizing a Trainium2 kernel written in Bass

```python
import math
import jax, jax.numpy as jnp
from jax import lax
import numpy as np

D_MODEL = 2048
BATCH = 4
SEQ = 2048
DEPTH = 1

N_META = 16
EXPAND = 2
D_MIX = EXPAND * D_MODEL
D_CONV_BR = D_MIX // 2
D_SSD = D_MIX - D_CONV_BR
SSD_HEAD_DIM = 64
SSD_HEADS = D_SSD // SSD_HEAD_DIM
SSD_GROUPS = 8
SSD_HEADS_PER_GROUP = SSD_HEADS // SSD_GROUPS
SSD_STATE = 128
SSD_CONV_WIDTH = 4
SSD_CHUNK = 128
D_XBC = D_SSD + 2 * SSD_GROUPS * SSD_STATE
CONF_WIDTH = 31
D_IN_PROJ = 3 * D_CONV_BR + D_SSD + D_XBC + SSD_HEADS
EPS = 1e-5

kernel_name = "hymba_conformer_ssd_hybrid"


def rmsnorm(x, w):
    xf = x.astype(jnp.float32)
    y = xf * lax.rsqrt(jnp.mean(xf * xf, axis=-1, keepdims=True) + EPS)
    return (y * w.astype(jnp.float32)).astype(x.dtype)


def layernorm(x, g, b):
    xf = x.astype(jnp.float32)
    mu = jnp.mean(xf, axis=-1, keepdims=True)
    var = jnp.mean(jnp.square(xf - mu), axis=-1, keepdims=True)
    y = (xf - mu) * lax.rsqrt(var + EPS)
    return (y * g.astype(jnp.float32) + b.astype(jnp.float32)).astype(x.dtype)


def causal_dwconv(u, w, b):
    k = w.shape[0]
    up = jnp.pad(u, ((0, 0), (k - 1, 0), (0, 0)))
    y = lax.conv_general_dilated(
        up, w[:, None, :].astype(u.dtype), window_strides=(1,), padding="VALID",
        dimension_numbers=("NWC", "WIO", "NWC"), feature_group_count=u.shape[-1])
    return y + b.astype(u.dtype)


def ssd_chunked(x, dt, A, Bm, Cm, Dsk):
    out_dtype = x.dtype
    f32 = jnp.float32
    bsz, seqlen = x.shape[0], x.shape[1]
    pad = (-N_META) % SSD_CHUNK
    padw = ((0, 0), (pad, 0), (0, 0), (0, 0))
    x = jnp.pad(x.astype(f32), padw)
    Bm = jnp.pad(Bm.astype(f32), padw)
    Cm = jnp.pad(Cm.astype(f32), padw)
    dt = jnp.pad(dt.astype(f32), ((0, 0), (pad, 0), (0, 0)))
    total = seqlen + pad
    nc = total // SSD_CHUNK
    G, R, P, N, Q = SSD_GROUPS, SSD_HEADS_PER_GROUP, SSD_HEAD_DIM, SSD_STATE, SSD_CHUNK

    x_c = x.reshape(bsz, nc, Q, G, R, P)
    xdt_c = x_c * dt.reshape(bsz, nc, Q, G, R)[..., None]
    dA_c = (dt * A.astype(f32)).reshape(bsz, nc, Q, G, R)
    B_c = Bm.reshape(bsz, nc, Q, G, N)
    C_c = Cm.reshape(bsz, nc, Q, G, N)

    Acum = jnp.cumsum(dA_c, axis=2)
    seg = Acum[:, :, :, None] - Acum[:, :, None]
    causal = jnp.tril(jnp.ones((Q, Q), dtype=bool))[:, :, None, None]
    Lmat = jnp.exp(jnp.where(causal, seg, -jnp.inf))

    CB = jnp.einsum("bcqgn,bcsgn->bcqsg", C_c, B_c)
    y_diag = jnp.einsum("bcqsg,bcqsgr,bcsgrp->bcqgrp", CB, Lmat, xdt_c)

    decay_states = jnp.exp(Acum[:, :, -1:] - Acum)
    states = jnp.einsum("bcsgn,bcsgr,bcsgrp->bcgrpn", B_c, decay_states, xdt_c)
    chunk_decay = jnp.exp(Acum[:, :, -1])

    def step(carry, inp):
        st, dec = inp
        new = carry * dec[..., None, None] + st
        return new, carry

    init = jnp.zeros((bsz, G, R, P, N), f32)
    _, prev = lax.scan(step, init, (jnp.moveaxis(states, 1, 0), jnp.moveaxis(chunk_decay, 1, 0)))
    prev = jnp.moveaxis(prev, 0, 1)

    y_off = jnp.einsum("bcqgn,bcgrpn,bcqgr->bcqgrp", C_c, prev, jnp.exp(Acum))
    y = y_diag + y_off + x_c * Dsk.astype(f32).reshape(G, R)[:, :, None]
    y = y.reshape(bsz, total, SSD_HEADS, P)[:, pad:]
    return y.astype(out_dtype)


def gated_group_rmsnorm(y, z, w):
    v = (y * jax.nn.silu(z)).astype(jnp.float32)
    shp = v.shape
    v = v.reshape(shp[:-1] + (SSD_GROUPS, shp[-1] // SSD_GROUPS))
    v = v * lax.rsqrt(jnp.mean(v * v, axis=-1, keepdims=True) + EPS)
    return (v.reshape(shp) * w.astype(jnp.float32)).astype(y.dtype)


def setup_inputs(seed: int = 0) -> dict:
    key = jax.random.key(seed)
    ks = jax.random.split(key, 16)
    nrm = jax.random.normal
    f32 = jnp.float32
    x = nrm(ks[0], (BATCH, SEQ, D_MODEL), f32)
    meta_tokens = nrm(ks[1], (N_META, D_MODEL), f32)
    norm_w = 1.0 + 0.02 * nrm(ks[2], (DEPTH, D_MODEL), f32)
    w_in = nrm(ks[3], (DEPTH, D_MODEL, D_IN_PROJ), f32) * D_MODEL ** -0.5
    conf_dw_w = nrm(ks[4], (DEPTH, CONF_WIDTH, D_CONV_BR), f32) * CONF_WIDTH ** -0.5
    conf_dw_b = 0.02 * nrm(ks[5], (DEPTH, D_CONV_BR), f32)
    conf_ln_g = 1.0 + 0.02 * nrm(ks[6], (DEPTH, D_CONV_BR), f32)
    conf_ln_b = 0.02 * nrm(ks[7], (DEPTH, D_CONV_BR), f32)
    ssd_conv_w = nrm(ks[8], (DEPTH, SSD_CONV_WIDTH, D_XBC), f32) * SSD_CONV_WIDTH ** -0.5
    ssd_conv_b = 0.02 * nrm(ks[9], (DEPTH, D_XBC), f32)
    u = jax.random.uniform(ks[10], (DEPTH, SSD_HEADS), f32)
    dt0 = jnp.exp(u * (math.log(0.1) - math.log(0.001)) + math.log(0.001))
    dt_bias = dt0 + jnp.log(-jnp.expm1(-dt0))
    A_log = jnp.log(jax.random.uniform(ks[11], (DEPTH, SSD_HEADS), f32, minval=1.0, maxval=16.0))
    D_skip = 1.0 + 0.02 * nrm(ks[12], (DEPTH, SSD_HEADS), f32)
    ssd_norm_w = 1.0 + 0.02 * nrm(ks[13], (DEPTH, D_SSD), f32)
    w_out = nrm(ks[14], (DEPTH, D_MIX, D_MODEL), f32) * D_MIX ** -0.5
    final_norm_w = 1.0 + 0.02 * nrm(ks[15], (D_MODEL,), f32)
    return {"x": x, "meta_tokens": meta_tokens, "norm_w": norm_w, "w_in": w_in,
            "conf_dw_w": conf_dw_w, "conf_dw_b": conf_dw_b, "conf_ln_g": conf_ln_g,
            "conf_ln_b": conf_ln_b, "ssd_conv_w": ssd_conv_w, "ssd_conv_b": ssd_conv_b,
            "dt_bias": dt_bias, "A_log": A_log, "D_skip": D_skip, "ssd_norm_w": ssd_norm_w,
            "w_out": w_out, "final_norm_w": final_norm_w}


def reference(x, meta_tokens, norm_w, w_in, conf_dw_w, conf_dw_b, conf_ln_g, conf_ln_b,
              ssd_conv_w, ssd_conv_b, dt_bias, A_log, D_skip, ssd_norm_w, w_out, final_norm_w):
    bsz = x.shape[0]
    meta = jnp.broadcast_to(meta_tokens[None].astype(x.dtype), (bsz, N_META, D_MODEL))
    h_stream = jnp.concatenate([meta, x], axis=1)
    L = h_stream.shape[1]
    splits = [D_CONV_BR, 2 * D_CONV_BR, 3 * D_CONV_BR, 3 * D_CONV_BR + D_SSD,
              3 * D_CONV_BR + D_SSD + D_XBC]
    for l in range(DEPTH):
        hn = rmsnorm(h_stream, norm_w[l])
        proj = hn @ w_in[l].astype(hn.dtype)
        c_val, c_gate, c_silu, z, xbc, dt_raw = jnp.split(proj, splits, axis=-1)

        u = c_val * jax.nn.sigmoid(c_gate)
        u = causal_dwconv(u, conf_dw_w[l], conf_dw_b[l])
        u = layernorm(u, conf_ln_g[l], conf_ln_b[l])
        y_conv = jax.nn.silu(u) * jax.nn.silu(c_silu)

        xbc = jax.nn.silu(causal_dwconv(xbc, ssd_conv_w[l], ssd_conv_b[l]))
        xs, Bm, Cm = jnp.split(xbc, [D_SSD, D_SSD + SSD_GROUPS * SSD_STATE], axis=-1)
        dt = jax.nn.softplus(dt_raw.astype(jnp.float32) + dt_bias[l].astype(jnp.float32))
        A = -jnp.exp(A_log[l].astype(jnp.float32))
        y_ssd = ssd_chunked(xs.reshape(bsz, L, SSD_HEADS, SSD_HEAD_DIM), dt, A,
                            Bm.reshape(bsz, L, SSD_GROUPS, SSD_STATE),
                            Cm.reshape(bsz, L, SSD_GROUPS, SSD_STATE), D_skip[l])
        y_ssd = gated_group_rmsnorm(y_ssd.reshape(bsz, L, D_SSD), z, ssd_norm_w[l])

        y = jnp.concatenate([y_conv, y_ssd.astype(y_conv.dtype)], axis=-1)
        h_stream = h_stream + (y @ w_out[l].astype(y.dtype)).astype(h_stream.dtype)
    out = rmsnorm(h_stream, final_norm_w)
    return out[:, N_META:]
```

```python
import numpy as np
import ml_dtypes
import concourse.bass as bass
import concourse.mybir as mybir
from concourse.bass_utils import run_bass_kernel_spmd

F32 = mybir.dt.float32
BF16 = mybir.dt.bfloat16
AF = mybir.ActivationFunctionType
ALU = mybir.AluOpType

D = 2048
KT = 16
TP = 2176
NTOK = 2064
PAD = 112
NT = 17
MT0 = 9
NMAIN = 8
EPS = 1e-5
PCE = "dve"
NG = 8


class Sched:
    ENGS = ("pe", "act", "dve", "pool", "sp")

    def __init__(self):
        self.ops = {e: [] for e in self.ENGS}
        self.last_w = {}
        self.readers = {}
        self.dma_cnt = {}

    def op(self, eng, fn, r=(), w=(), dma=None):
        def _n(x):
            return x[:3] if x.startswith("ps") else x
        w = [_n(x) for x in w] + [_n(x) for x in r if x.startswith("ps")]
        r = [x for x in r if not x.startswith("ps")]
        idx = len(self.ops[eng])
        deps = set()
        for x in r:
            if x in self.last_w:
                deps.add(self.last_w[x])
        for x in w:
            if x in self.last_w:
                deps.add(self.last_w[x])
            rd = self.readers.get(x)
            if rd:
                deps.update(rd.values())
        if dma is not None:
            self.dma_cnt[dma] = self.dma_cnt.get(dma, 0) + 1
            tok = ("dma", dma, 16 * self.dma_cnt[dma])
        else:
            tok = ("eng", eng, idx)
        if eng == "pe" and dma is None:
            deps = {d for d in deps if not (d[0] == "eng" and d[1] == "pe")}
        for x in w:
            self.last_w[x] = tok
            self.readers[x] = {}
        for x in r:
            key = eng if dma is None else ("dma", dma, self.dma_cnt[dma])
            self.readers.setdefault(x, {})[key] = tok
        self.ops[eng].append(dict(fn=fn, deps=deps, tok=tok, dma=dma))
        return tok

    def barrier(self):
        toks = set()
        for e in self.ENGS:
            for o in reversed(self.ops[e]):
                if o["dma"] is None:
                    toks.add(o["tok"])
                    break
        for ch, c in self.dma_cnt.items():
            toks.add(("dma", ch, 16 * c))
        for e in self.ENGS:
            self.ops[e].append(dict(fn=None, deps=set(toks), tok=("eng", e, len(self.ops[e])), dma=None))

    def emit(self, nc):
        targets = {e: set() for e in self.ENGS}
        for e in self.ENGS:
            for o in self.ops[e]:
                for d in o["deps"]:
                    if d[0] == "eng":
                        targets[d[1]].add(d[2])
        counts = {}
        for e in self.ENGS:
            c = 0
            cl = []
            for i, o in enumerate(self.ops[e]):
                if i in targets[e] and o["dma"] is None:
                    c += 1
                cl.append(c)
            counts[e] = cl
        import contextlib
        with contextlib.ExitStack() as st:
            esem = {e: st.enter_context(nc.semaphore("s_" + e)) for e in self.ENGS}
            dsem = {ch: st.enter_context(nc.semaphore("d_" + str(ch))) for ch in self.dma_cnt}
            block = st.enter_context(nc.Block())

            def run(e, h):
                seen = {}
                for i, o in enumerate(self.ops[e]):
                    for d in sorted(o["deps"], key=str):
                        if d[0] == "eng":
                            sem, val, key = esem[d[1]], counts[d[1]][d[2]], ("e", d[1])
                        else:
                            sem, val, key = dsem[d[1]], d[2], ("d", d[1])
                        if seen.get(key, 0) >= val:
                            continue
                        seen[key] = val
                        h.wait_ge(sem, val)
                    if o["fn"] is None:
                        ins = None
                        if i in targets[e]:
                            ins = h.nop()
                    else:
                        ins = o["fn"](h)
                    if o["dma"] is not None:
                        ins.then_inc(dsem[o["dma"]], 16)
                    elif i in targets[e]:
                        ins.then_inc(esem[e], 1)

            @block.tensor
            def _(h):
                run("pe", h)

            @block.scalar
            def _(h):
                run("act", h)

            @block.vector
            def _(h):
                run("dve", h)

            @block.gpsimd
            def _(h):
                run("pool", h)

            @block.sync
            def _(h):
                run("sp", h)


def build(stage=99, dbg=False):
    nc = bass.Bass("TRN2", target_bir_lowering=False)
    S = Sched()

    def din(name, shape, dt=F32):
        return nc.dram_tensor(name, list(shape), dt, kind="ExternalInput").ap()

    xin = din("xin", [TP, D])
    maskd = din("mask", [128, NT])
    normw_d = din("normw", [128, KT])
    wdt_d = din("w_dt", [128, KT, 32])
    wssd_d = din("w_ssd", [NG, 3, 128, KT, 256])
    wcv_d = din("w_cv", [16, 128, KT, 256])
    wcs_d = din("w_cs", [16, 128, KT, 128])
    wout_d = din("w_out", [16, 128, 32, 128])
    cw4_d = din("cw4", [128, NG * 16])
    cb4_d = din("cb4", [128, NG * 4])
    cw31_d = din("cw31", [128, 16 * 31])
    cb31_d = din("cb31", [128, 16])
    lng_d = din("lng", [128, 16])
    lnb_d = din("lnb", [128, 16])
    dtb_d = din("dtb", [128, 32])
    alog_d = din("alog", [128, 32])
    dsk_d = din("dsk", [128, 32])
    snw_d = din("snw", [128, D])
    fnw_d = din("fnw", [128, D])
    ident_d = din("ident", [128, 128], BF16)
    onesb_d = din("onesb", [128, 128], BF16)
    m01_d = din("m01", [128, 128], BF16)
    onesf_d = din("onesf", [128, 128])
    U_d = din("U", [128, 128])
    out_d = nc.dram_tensor("out", [1024, D], F32, kind="ExternalOutput").ap()
    dbg_outs = {}

    BASE = 16512
    off = [BASE]

    def sb(name, shape, dt, at=None):
        esz = 4 if dt == F32 else 2
        n = 1
        for s in shape[1:]:
            n *= s
        nbytes = (n * esz + 31) // 32 * 32
        if at is None:
            o = off[0]
            off[0] += nbytes
        else:
            o = at
        t = nc.alloc_sbuf_tensor_at(name, list(shape), dt, offset=o)
        return t, o, nbytes

    hnT, _, _ = sb("hnT", [128, KT, NTOK], BF16)
    hn0, _hn0o, _ = sb("hn0", [128, KT, 128], BF16)
    HN0_OFF = [_hn0o]
    T0 = off[0]
    off[0] += 40960
    Y0 = off[0]
    yT, _, _ = sb("yT", [128, 32, 1024], BF16)
    _w = [sb("wsl%d" % i, [128, 4096], BF16) for i in range(3)]
    wsl = [w_[0] for w_ in _w]
    CEND_W2 = [_w[2][1]]
    ident, _, _ = sb("identS", [128, 128], BF16)
    onesb, _, _ = sb("onesbS", [128, 128], BF16)
    m01, _, _ = sb("m01S", [128, 128], BF16)
    onesf, _, _ = sb("onesfS", [128, 128], F32)
    Um, _, _ = sb("US", [128, 128], F32)
    normw, _, _ = sb("normwS", [128, KT], F32)
    maskS, _, _ = sb("maskS", [128, NT], F32)
    cw4, _, _ = sb("cw4S", [128, NG * 16], F32)
    cb4, _, _ = sb("cb4S", [128, NG * 4], F32)
    cw31, _, _ = sb("cw31S", [128, 16 * 31], F32)
    cb31, _, _ = sb("cb31S", [128, 16], F32)
    lng, _, _ = sb("lngS", [128, 16], F32)
    lnb, _, _ = sb("lnbS", [128, 16], F32)
    dtb, _, _ = sb("dtbS", [128, 32], F32)
    Aneg, _, _ = sb("AnegS", [128, 32], F32)
    dsk, _, _ = sb("dskS", [128, 32], F32)
    wdt, _, _ = sb("wdtS", [128, KT, 32], BF16)
    dtA, _, _ = sb("dtA", [128, NT, 32], F32)
    dAA, _, _ = sb("dAA", [128, NT, 32], F32)
    sml, _, _ = sb("sml", [128, 64], F32)
    CEND = off[0]
    assert CEND <= 229344, CEND

    PP = [nc.alloc_psum_tensor("pp%d" % i, [128, 1024], F32) for i in range(4)]

    def bank(b):
        return PP[b // 2][:, (b % 2) * 512:(b % 2) * 512 + 512]

    def bankbf(b):
        return PP[b // 2][:, (b % 2) * 512:(b % 2) * 512 + 512].bitcast(BF16)

    def PB(b):
        return "ps%d" % b

    cl = [(ident, ident_d), (onesb, onesb_d), (m01, m01_d), (onesf, onesf_d), (Um, U_d), (normw, normw_d),
          (maskS, maskd), (cw4, cw4_d), (cb4, cb4_d), (cw31, cw31_d), (cb31, cb31_d), (lng, lng_d),
          (lnb, lnb_d), (dtb, dtb_d), (Aneg, alog_d), (dsk, dsk_d)]
    for i, (s_, d_) in enumerate(cl):
        S.op("sp", lambda e, s_=s_, d_=d_: e.dma_start(out=s_[:, :], in_=d_), w=["const%d" % i], dma="const")
    CONST = ["const%d" % i for i in range(len(cl))]
    S.op("pool", lambda e: e.dma_start(out=wdt[:, :, :], in_=wdt_d),
         w=["wdt"], dma="wdt")
    S.op("act", lambda e: e.activation(out=Aneg[:, :], in_=Aneg[:, :], func=AF.Exp), r=CONST, w=["Aneg"])
    S.op("dve", lambda e: e.tensor_scalar(out=Aneg[:, :], in0=Aneg[:, :], scalar1=-1.0, scalar2=None,
                                          op0=ALU.mult), r=["Aneg"], w=["Aneg"])

    xst = [sb("xst%d" % i, [128, D], F32, at=T0 + (i * 8192 if i < 2 else 29184))[0] for i in range(3)]
    xs = [sb("xs%d" % i, [128, D], BF16, at=T0 + 16384 + i * 4096)[0] for i in range(2)]
    junk, _, _ = sb("junk", [128, D], BF16, at=T0 + 24576)
    ssA, _, _ = sb("ssA", [128, NT * 4], F32, at=T0 + 28672)
    def phA1(t):
        sl = t % 3
        sl2 = t % 2
        S.op("sp", lambda e, t=t, sl=sl, sl2=sl2: e.dma_start(out=xst[sl][:, 0:1024], in_=xin[t * 128:(t + 1) * 128, 0:1024]),
             w=["xstA%d" % sl], dma="x%d" % sl)
        S.op("act", lambda e, t=t, sl=sl, sl2=sl2: e.dma_start(out=xst[sl][:, 1024:2048], in_=xin[t * 128:(t + 1) * 128, 1024:2048]),
             w=["xstB%d" % sl], dma="xb%d" % sl)
        S.op("act", lambda e, t=t, sl=sl, sl2=sl2: e.activation(out=junk[:, :], in_=xst[sl][:, :], func=AF.Square,
                                                      accum_out=ssA[:, 4 * t:4 * t + 1]),
             r=["xstA%d" % sl, "xstB%d" % sl], w=["junk", "ssA%d" % t])
        S.op("dve", lambda e, t=t: e.tensor_scalar(out=ssA[:, 4 * t + 1:4 * t + 2], in0=ssA[:, 4 * t:4 * t + 1],
                                                  scalar1=1.0 / D, scalar2=EPS, op0=ALU.mult, op1=ALU.add),
             r=["ssA%d" % t], w=["ssB%d" % t])
    def phA2(t):
        sl = t % 3
        sl2 = t % 2
        S.op("act", lambda e, t=t: e.activation(out=ssA[:, 4 * t + 2:4 * t + 3], in_=ssA[:, 4 * t + 1:4 * t + 2],
                                               func=AF.Sqrt), r=["ssB%d" % t], w=["ssC%d" % t])
        S.op("dve", lambda e, t=t: e.reciprocal(out=ssA[:, 4 * t + 3:4 * t + 4], in_=ssA[:, 4 * t + 2:4 * t + 3]),
             r=["ssC%d" % t], w=["ssD%d" % t])
        S.op("act", lambda e, t=t, sl=sl, sl2=sl2: e.activation(out=xs[sl2][:, :], in_=xst[sl][:, :], func=AF.Copy,
                                                      scale=ssA[:, 4 * t + 3:4 * t + 4]),
             r=["xstA%d" % sl, "xstB%d" % sl, "ssD%d" % t], w=["xs%d" % sl2])
        b0 = 2 * sl2
        for k in range(KT):
            bb = b0 + (k // 8)
            S.op("pe", lambda e, k=k, bb=bb, sl=sl, sl2=sl2: e.transpose(out=bankbf(bb)[:, (k % 8) * 128:(k % 8) * 128 + 128],
                                                              in_=xs[sl2][:, k * 128:(k + 1) * 128],
                                                              identity=ident[:, :]),
                 r=["xs%d" % sl2] + CONST, w=[PB(bb)])
        for hh in range(2):
            bb = b0 + hh
            src = bankbf(bb).rearrange("p (k c) -> p k c", k=8)
            nwb = normw[:, hh * 8:(hh + 1) * 8].unsqueeze(2)
            if t == 0:
                S.op("dve", lambda e, src=src, nwb=nwb, hh=hh: e.tensor_tensor(
                    out=hn0[:, hh * 8:(hh + 1) * 8, :], in0=src, in1=nwb.broadcast_to([128, 8, 128]), op=ALU.mult),
                    r=[PB(bb)] + CONST, w=["hn0"])
                S.op("dve", lambda e, src=src, nwb=nwb, hh=hh: e.tensor_tensor(
                    out=hnT[:, hh * 8:(hh + 1) * 8, 0:16], in0=src[:, :, PAD:128],
                    in1=nwb.broadcast_to([128, 8, 16]), op=ALU.mult),
                    r=[PB(bb)] + CONST, w=["hnT%d" % t])
            else:
                c0 = t * 128 - PAD
                S.op("dve", lambda e, src=src, nwb=nwb, hh=hh, c0=c0: e.tensor_tensor(
                    out=hnT[:, hh * 8:(hh + 1) * 8, c0:c0 + 128], in0=src,
                    in1=nwb.broadcast_to([128, 8, 128]), op=ALU.mult),
                    r=[PB(bb)] + CONST, w=["hnT%d" % t])


    phA1(0)
    for t in range(NT):
        if t + 1 < NT:
            phA1(t + 1)
        phA2(t)
    def hn_res(t0, t1):
        return ["hnT%d" % t for t in range(t0, t1)]

    def hcols(t):
        return None

    def lhs_tok(k, t):
        if t == 0:
            return hn0[:, k, :]
        return hnT[:, k, t * 128 - PAD:t * 128 - PAD + 128]

    S.barrier()
    for t in range(NT):
        for k in range(KT):
            S.op("pe", lambda e, k=k, t=t: e.matmul(bank(4)[:, 0:32], lhsT=lhs_tok(k, t), rhs=wdt[:, k, :],
                                                   start=(k == 0), stop=(k == KT - 1)),
                 r=["hnT%d" % t, "hn0", "wdt"], w=[PB(4)])
        S.op("dve", lambda e, t=t: e.tensor_tensor(out=dtA[:, t, :], in0=bank(4)[:, 0:32], in1=dtb[:, :], op=ALU.add),
             r=[PB(4)] + CONST, w=["dtA%d" % t])
        S.op("act", lambda e, t=t: e.activation(out=dtA[:, t, :], in_=dtA[:, t, :], func=AF.Exp),
             r=["dtA%d" % t], w=["dtA%d" % t])
        S.op("act", lambda e, t=t: e.activation(out=dtA[:, t, :], in_=dtA[:, t, :], func=AF.Ln, bias=1.0),
             r=["dtA%d" % t], w=["dtA%d" % t])
        S.op("dve", lambda e, t=t: e.tensor_scalar(out=dtA[:, t, :], in0=dtA[:, t, :], scalar1=maskS[:, t:t + 1],
                                                  scalar2=None, op0=ALU.mult),
             r=["dtA%d" % t] + CONST, w=["dtA%d" % t])
        S.op("dve", lambda e, t=t: e.tensor_tensor(out=dAA[:, t, :], in0=dtA[:, t, :], in1=Aneg[:, :], op=ALU.mult),
             r=["dtA%d" % t, "Aneg"], w=["dAA%d" % t])


    negAcA, _, _ = sb("negAcA", [128, NT, 32], F32, at=T0 + 33792)
    dsA, _, _ = sb("dsA", [128, NT, 32], F32, at=T0 + 33792 + 2176)
    cdA, _, _ = sb("cdA", [128, NT, 32], F32, at=T0 + 33792 + 4352)
    dAres = ["dAA%d" % t for t in range(NT)]
    for (c0, n) in ((0, 16), (16, 1)):
        src = dAA[:, c0:c0 + n, :].rearrange("p c h -> p (c h)")
        S.op("pe", lambda e, src=src, n=n: e.matmul(bank(5)[:, 0:n * 32], lhsT=Um[:, :], rhs=src, start=True, stop=True),
             r=dAres + CONST, w=[PB(5)])
        S.op("pe", lambda e, src=src, n=n: e.matmul(bank(6)[:, 0:n * 32], lhsT=onesf[:, :], rhs=src, start=True, stop=True),
             r=dAres + CONST, w=[PB(6)])
        dst = negAcA[:, c0:c0 + n, :].rearrange("p c h -> p (c h)")
        S.op("dve", lambda e, dst=dst, n=n: e.tensor_scalar(out=dst, in0=bank(5)[:, 0:n * 32], scalar1=-1.0, scalar2=None,
                                                           op0=ALU.mult), r=[PB(5)], w=["negAcA"])
        dd = dsA[:, c0:c0 + n, :].rearrange("p c h -> p (c h)")
        S.op("dve", lambda e, dd=dd, dst=dst, n=n: e.tensor_tensor(out=dd, in0=bank(6)[:, 0:n * 32], in1=dst, op=ALU.add),
             r=[PB(6), "negAcA"], w=["dsA"])
        if c0 == 0:
            S.op("dve", lambda e: e.memset(sml[:, 0:32], 0.0), w=["sml"])
            for c_ in range(MT0 - 1, -1, -1):
                if c_ < MT0 - 1:
                    S.op("dve", lambda e, c_=c_: e.tensor_tensor(out=dsA[:, c_, :], in0=dsA[:, c_, :], in1=sml[:, 0:32],
                                                                op=ALU.add), r=["dsA", "sml"], w=["dsA"])
                if c_ > 0:
                    S.op("dve", lambda e, c_=c_: e.tensor_tensor(out=sml[:, 0:32], in0=bank(6)[:, c_ * 32:(c_ + 1) * 32],
                                                                in1=sml[:, 0:32], op=ALU.add), r=[PB(6), "sml"], w=["sml"])
        S.op("act", lambda e, dd=dd: e.activation(out=dd, in_=dd, func=AF.Exp), r=["dsA"], w=["dsA"])
        cc_ = cdA[:, c0:c0 + n, :].rearrange("p c h -> p (c h)")
        S.op("act", lambda e, cc_=cc_, n=n: e.activation(out=cc_, in_=bank(6)[:, 0:n * 32], func=AF.Exp),
             r=[PB(6)], w=["cdA"])

    dumps = []

    def dump(name, ap, shape, res, dt=F32):
        d = nc.dram_tensor("dbg_" + name, list(shape), dt, kind="ExternalOutput").ap()
        dbg_outs[name] = d
        S.op("sp", lambda e, ap=ap, d=d: e.dma_start(out=d, in_=ap), r=res, w=["dbgout_" + name], dma="dbg_" + name)

    if dbg and stage <= 1:
        dump("hnT", hnT[:, :, :], [128, KT, NTOK], hn_res(0, NT), BF16)
        dump("dtA", dtA[:, :, :], [128, NT, 32], ["dtA%d" % t for t in range(NT)])
        dump("dAA", dAA[:, :, :], [128, NT, 32], ["dAA%d" % t for t in range(NT)])

    wchan = [0]

    preloaded = set()

    def wload(slot, src_ap, view, key=None):
        if key is not None:
            if key in preloaded:
                return
            preloaded.add(key)
        S.op("pool", lambda e, slot=slot, src_ap=src_ap, view=view: e.dma_start(out=view, in_=src_ap, max_dma_last_dim=8192),
             w=["wsl%d" % slot], dma="w%d" % slot)

    if stage >= 2:
        o = T0
        rawf, _, n_ = sb("rawf", [128, 3 + TP], BF16, at=o); o += n_
        xact, _, n_ = sb("xact", [128, TP], BF16, at=o); o += n_
        BT, _, n_ = sb("BT", [128, TP], BF16, at=o); o += n_
        CT, _, n_ = sb("CT", [128, TP], BF16, at=o); o += n_
        xtok, _, n_ = sb("xtok", [128, NT, 256], BF16, at=o); o += n_
        Btok, _, n_ = sb("Btok", [128, NT, 128], BF16, at=o); o += n_
        dg4, _, n_ = sb("dg4", [128, 4, 128], BF16, at=o); o += n_
        assert o <= T0 + 40960, o
        o = Y0
        def two(name, shape, dt):
            nonlocal_o = []
            return None
        bufs2 = {}
        def alloc2(name, shape, dt):
            nonlocal o
            lst = []
            for i_ in range(2):
                t_, _, n_ = sb(name + str(i_), shape, dt, at=o); o += n_
                lst.append(t_)
            return lst
        rhsA = alloc2("rhsA", [128, 4, 128], F32)
        segm = alloc2("segm", [128, 4, 128], F32)
        Lm = alloc2("Lm", [128, 4, 128], BF16)
        eAb = alloc2("eAb", [128, 4, 128], BF16)
        Mm = alloc2("Mm", [128, 4, 128], BF16)
        CTs = alloc2("CTs", [128, 4, 128], BF16)
        CBm = alloc2("CBm", [128, 128], BF16)
        xdd = alloc2("xdd", [128, 4, 64], BF16)
        prevbf = alloc2("prevbf", [128, 4, 64], BF16)
        szz = alloc2("szz", [128, 256], F32)
        vn = alloc2("vn", [128, 256], BF16)
        xDall, _, n_ = sb("xDall", [128, NMAIN, 256], BF16, at=o); o += n_
        vall, _, n_ = sb("vall", [128, NMAIN, 256], BF16, at=o); o += n_
        prev32, _, n_ = sb("prev32", [128, 4, 64], F32, at=o); o += n_
        gsA, _, n_ = sb("gsA", [128, 32], F32, at=o); o += n_
        nwg, _, n_ = sb("nwg", [128, 256], F32, at=o); o += n_
        assert o <= Y0 + 32768, o
        junk2, _, _ = sb("junk2", [128, 256], BF16, at=T0 + 31520)
        vn4 = [vn[0], vn[1], sb("vn2", [128, 256], BF16, at=T0 + 32032)[0], sb("vn3", [128, 256], BF16, at=T0 + 32544)[0]]

        S.op("dve", lambda e: e.memset(rawf[:, 0:3 + PAD], 0.0), w=["rawf_pad"])
        tblocks = [(0, 16)] + [(16 + i * 512, 512) for i in range(4)]
        pblocks = [(i * 512, 512) for i in range(4)] + [(2048, 128)]
        ipb = [0]
        for g in range(NG):
            wx = wsl[0][:, :].rearrange("p (k c) -> p k c", k=KT)
            wbc = wsl[1][:, :].rearrange("p (k c) -> p k c", k=KT)
            wz = wsl[2][:, :].rearrange("p (k c) -> p k c", k=KT)
            wload(0, wssd_d[g, 0], wx)
            wload(1, wssd_d[g, 1], wbc)
            wload(2, wssd_d[g, 2], wz)
            S.op("sp", lambda e, g=g: e.dma_start(out=nwg[:, :], in_=snw_d[:, g * 256:(g + 1) * 256]),
                 w=["nwg"], dma="nwg")
            for ct in range(4):
                wv = (wx, wx, wbc, wbc)[ct]
                wslot = (0, 0, 1, 1)[ct]
                cc = (0, 128, 0, 128)[ct]
                for kk in range(4):
                    S.op("dve", lambda e, g=g, ct=ct, kk=kk: e.tensor_scalar(
                        out=dg4[:, kk, :], in0=ident[:, :],
                        scalar1=cw4[:, g * 16 + ct * 4 + kk:g * 16 + ct * 4 + kk + 1], scalar2=None, op0=ALU.mult),
                        r=CONST, w=["dg4"])
                def segs(prefix, lo, hi):
                    return ["%s%d" % (prefix, i_) for i_ in range(max(lo, 0) // 512, (hi - 1) // 512 + 1)]
                if ct == 3:
                    tbl = [(1024, 16), (1040, 512), (1552, 512)]
                    pbl = [(1152, 512), (1664, 512)]
                else:
                    tbl, pbl = tblocks, pblocks
                for (t0, n) in tbl:
                    pb = ipb[0] % 2
                    ipb[0] += 1
                    for k in range(KT):
                        S.op("pe", lambda e, k=k, t0=t0, n=n, pb=pb, wv=wv, cc=cc: e.matmul(
                            bank(pb)[:, 0:n], lhsT=wv[:, k, cc:cc + 128], rhs=hnT[:, k, t0:t0 + n],
                            start=(k == 0), stop=(k == KT - 1)),
                            r=["wsl%d" % wslot] + hn_res(0, NT), w=[PB(pb)])
                    S.op("act", lambda e, t0=t0, n=n, pb=pb: e.activation(
                        out=rawf[:, 3 + PAD + t0:3 + PAD + t0 + n], in_=bank(pb)[:, 0:n], func=AF.Copy),
                        r=[PB(pb)], w=segs("rawf_s", t0 + PAD, t0 + PAD + n))
                dst = (xact, xact, BT, CT)[ct]
                dname = ("xact", "xact", "BT", "CT")[ct]
                for pi_, (p0, n) in enumerate(pbl):
                    cb_ = 2 + (pi_ % 2)
                    need = (["rawf_pad"] if p0 == 0 else []) + segs("rawf_s", p0 - 3, p0 + n)
                    for kk in range(4):
                        S.op("pe", lambda e, kk=kk, p0=p0, n=n, cb_=cb_: e.matmul(
                            bank(cb_)[:, 0:n], lhsT=dg4[:, kk, :], rhs=rawf[:, p0 + kk:p0 + kk + n],
                            start=(kk == 0), stop=(kk == 3)), r=["dg4"] + need, w=[PB(cb_)])
                    S.op("act", lambda e, p0=p0, n=n, dst=dst, g=g, ct=ct, cb_=cb_: e.activation(
                        out=dst[:, p0:p0 + n], in_=bank(cb_)[:, 0:n], func=AF.Silu,
                        bias=cb4[:, g * 4 + ct:g * 4 + ct + 1]), r=[PB(cb_)] + CONST, w=segs(dname + "_p", p0, p0 + n))
                if ct <= 2:
                    srcT = xact if ct < 2 else BT
                    sname = "xact" if ct < 2 else "BT"
                    for tq in range(0, NT, 4):
                        nn = min(4, NT - tq)
                        for j in range(nn):
                            S.op("pe", lambda e, tq=tq, j=j, srcT=srcT: e.transpose(
                                out=bankbf(4)[:, j * 128:(j + 1) * 128], in_=srcT[:, (tq + j) * 128:(tq + j + 1) * 128],
                                identity=ident[:, :]), r=["%s_p%d" % (sname, tq // 4)] + CONST, w=[PB(4)])
                        if ct < 2:
                            S.op("dve", lambda e, tq=tq, nn=nn, ct=ct: e.tensor_copy(
                                out=xtok[:, tq:tq + nn, ct * 128:(ct + 1) * 128],
                                in_=bankbf(4)[:, 0:nn * 128].rearrange("p (j c) -> p j c", j=nn)),
                                r=[PB(4)], w=["xtok"])
                        else:
                            S.op("dve", lambda e, tq=tq, nn=nn: e.tensor_copy(
                                out=Btok[:, tq:tq + nn, :],
                                in_=bankbf(4)[:, 0:nn * 128].rearrange("p (j c) -> p j c", j=nn)),
                                r=[PB(4)], w=["Btok"])
            if dbg and stage == 2 and g == 0:
                dump("xtok", xtok[:, :, :], [128, NT, 256], ["xtok"], BF16)
                dump("Btok", Btok[:, :, :], [128, NT, 128], ["Btok"], BF16)
                dump("CT", CT[:, :], [128, TP], ["CT_p%d" % i_ for i_ in range(5)], BF16)
            S.op(PCE, lambda e, g=g: e.tensor_tensor(
                out=xDall[:, :, :].rearrange("p c (r q) -> p c r q", r=4),
                in0=xtok[:, MT0:NT, :].rearrange("p c (r q) -> p c r q", r=4),
                in1=dsk[:, g * 4:(g + 1) * 4].unsqueeze(1).unsqueeze(3).broadcast_to([128, NMAIN, 4, 64]), op=ALU.mult),
                r=["xtok"] + CONST, w=["xDall"])
            S.op(PCE, lambda e, g=g: e.tensor_tensor(
                out=xtok[:, :, :].rearrange("p c (r q) -> p c r q", r=4),
                in0=xtok[:, :, :].rearrange("p c (r q) -> p c r q", r=4),
                in1=dtA[:, :, g * 4:(g + 1) * 4].unsqueeze(3).broadcast_to([128, NT, 4, 64]), op=ALU.mult),
                r=["xtok", "xDall"] + ["dtA%d" % t for t in range(NT)], w=["xtok"])
            S.op(PCE, lambda e, g=g: e.tensor_tensor(
                out=vall[:, :, :].rearrange("p c (r q) -> p c r q", r=4),
                in0=xtok[:, 0:8, :].rearrange("p c (r q) -> p c r q", r=4),
                in1=dsA[:, 0:8, g * 4:(g + 1) * 4].unsqueeze(3).broadcast_to([128, 8, 4, 64]), op=ALU.mult),
                r=["xtok", "dsA"], w=["vall%d" % c_ for c_ in range(MT0, NT)])
            S.op(PCE, lambda e, g=g: e.tensor_tensor(
                out=xdd[0][:, :, :], in0=xtok[:, 8, :].rearrange("p (r q) -> p r q", r=4),
                in1=dsA[:, 8, g * 4:(g + 1) * 4].unsqueeze(2).broadcast_to([128, 4, 64]), op=ALU.mult),
                r=["xtok", "dsA"], w=["xdd0"])
            for c_ in range(MT0):
                rhs_ = vall[:, c_, :] if c_ < 8 else xdd[0][:, :, :].rearrange("p r q -> p (r q)")
                rn = ("vall%d" % (c_ + MT0)) if c_ < 8 else "xdd0"
                S.op("pe", lambda e, c_=c_, rhs_=rhs_: e.matmul(bank(7)[:, 0:256], lhsT=Btok[:, c_, :], rhs=rhs_,
                                                              start=(c_ == 0), stop=(c_ == MT0 - 1)),
                     r=["Btok", rn], w=[PB(7)])
            S.op("dve", lambda e: e.tensor_copy(out=prev32[:, :, :], in_=bank(7)[:, 0:256].rearrange("p (r q) -> p r q", r=4)),
                 r=[PB(7)], w=["prev32"])
            S.op("act", lambda e: e.activation(out=prevbf[MT0 % 2][:, :, :], in_=prev32[:, :, :], func=AF.Copy),
                 r=["prev32"], w=["prevbf%d" % (MT0 % 2)])

            def stA(c, g=g):
                pa = c % 2
                dAg = dAA[:, c, g * 4:(g + 1) * 4]
                S.op(PCE, lambda e: e.tensor_tensor(
                    out=rhsA[pa][:, :, :], in0=Um[:, :].unsqueeze(1).broadcast_to([128, 4, 128]),
                    in1=dAg.unsqueeze(2).broadcast_to([128, 4, 128]), op=ALU.mult),
                    r=["dAA%d" % c] + CONST, w=["rhsA%d" % pa])

            def stBC(c, g=g):
                pa = c % 2
                S.op("pe", lambda e: e.matmul(bank(pa)[:, :], lhsT=onesf[:, :],
                                              rhs=rhsA[pa][:, :, :].rearrange("p r q -> p (r q)"), start=True, stop=True),
                     r=["rhsA%d" % pa] + CONST, w=[PB(pa)])
                S.op("pe", lambda e: e.matmul(bank(6)[:, pa * 128:(pa + 1) * 128], lhsT=BT[:, c * 128:(c + 1) * 128],
                                              rhs=CT[:, c * 128:(c + 1) * 128], start=True, stop=True),
                     r=["BT_p%d" % i_ for i_ in range(5)] + ["CT_p%d" % i_ for i_ in range(5)], w=[PB(6)])
                S.op("dve", lambda e: e.tensor_tensor(
                    out=segm[pa][:, :, :], in0=bank(pa)[:, :].rearrange("p (r q) -> p r q", r=4),
                    in1=negAcA[:, c, g * 4:(g + 1) * 4].unsqueeze(2).broadcast_to([128, 4, 128]), op=ALU.add),
                    r=[PB(pa), "negAcA"], w=["segm%d" % pa])
                S.op("act", lambda e: e.activation(out=eAb[pa][:, :, :].rearrange("p r q -> p (r q)"), in_=bank(pa)[:, :],
                                                   func=AF.Exp), r=[PB(pa)], w=["eAb%d" % pa])
                S.op("dve", lambda e: e.tensor_tensor(out=CBm[pa][:, :], in0=bank(6)[:, pa * 128:(pa + 1) * 128],
                                                     in1=m01[:, :], op=ALU.mult),
                     r=[PB(6)] + CONST, w=["CBm%d" % pa])
                S.op("act", lambda e: e.activation(out=Lm[pa][:, :, :], in_=segm[pa][:, :, :], func=AF.Exp),
                     r=["segm%d" % pa], w=["Lm%d" % pa])

            def stDE(c, g=g):
                pa = c % 2
                S.op("dve", lambda e: e.scalar_tensor_tensor(
                    out=Mm[pa][:, :, :], in0=Lm[pa][:, :, :], scalar=1.0,
                    in1=CBm[pa][:, :].unsqueeze(1).broadcast_to([128, 4, 128]), op0=ALU.min, op1=ALU.mult),
                    r=["Lm%d" % pa, "CBm%d" % pa], w=["Mm%d" % pa])
                S.op(PCE, lambda e: e.tensor_tensor(
                    out=CTs[pa][:, :, :], in0=eAb[pa][:, :, :],
                    in1=CT[:, c * 128:(c + 1) * 128].unsqueeze(1).broadcast_to([128, 4, 128]), op=ALU.mult),
                    r=["eAb%d" % pa] + ["CT_p%d" % i_ for i_ in range(5)], w=["CTs%d" % pa])
                for k in range(KT):
                    S.op("pe", lambda e, k=k: e.matmul(bank(2 + pa)[:, 0:256], lhsT=lhs_tok(k, c), rhs=wz[:, k, :],
                                                       start=(k == 0), stop=(k == KT - 1)),
                         r=["wsl2", "hnT%d" % c], w=[PB(2 + pa)])
                S.op("act", lambda e: e.activation(out=szz[pa][:, :], in_=bank(2 + pa)[:, 0:256], func=AF.Tanh, scale=0.5),
                     r=[PB(2 + pa)], w=["szz%d" % pa])
                S.op("dve", lambda e: e.scalar_tensor_tensor(out=szz[pa][:, :], in0=szz[pa][:, :], scalar=1.0,
                                                            in1=bank(2 + pa)[:, 0:256], op0=ALU.add, op1=ALU.mult),
                     r=["szz%d" % pa, PB(2 + pa)], w=["szz%d" % pa])

            def stS(c, g=g):
                pa = c % 2
                S.op(PCE, lambda e: e.tensor_tensor(
                    out=xdd[pa][:, :, :], in0=xtok[:, c, :].rearrange("p (r q) -> p r q", r=4),
                    in1=dsA[:, c, g * 4:(g + 1) * 4].unsqueeze(2).broadcast_to([128, 4, 64]), op=ALU.mult),
                    r=["xtok", "dsA"], w=["xdd%d" % pa])
                S.op("pe", lambda e: e.matmul(bank(7)[:, pa * 256:(pa + 1) * 256], lhsT=Btok[:, c, :],
                                              rhs=xdd[pa][:, :, :].rearrange("p r q -> p (r q)"),
                                              start=True, stop=True), r=["Btok", "xdd%d" % pa], w=[PB(7)])

            def stF(c, g=g):
                pa = c % 2
                main = c >= MT0
                if main:
                    yb = bank(4 + pa)
                    S.op("pe", lambda e: e.matmul(yb[:, 0:256], lhsT=ident[:, :], rhs=xDall[:, c - MT0, :],
                                                  start=True, stop=False), r=["xDall"] + CONST, w=[PB(4 + pa)])
                    for r_ in range(4):
                        S.op("pe", lambda e, r_=r_: e.matmul(yb[:, r_ * 64:(r_ + 1) * 64], lhsT=Mm[pa][:, r_, :],
                                                            rhs=xtok[:, c, r_ * 64:(r_ + 1) * 64], start=False, stop=False),
                             r=["Mm%d" % pa, "xtok"], w=[PB(4 + pa)])
                    for r_ in range(4):
                        S.op("pe", lambda e, r_=r_: e.matmul(yb[:, r_ * 64:(r_ + 1) * 64], lhsT=CTs[pa][:, r_, :],
                                                            rhs=prevbf[pa][:, r_, :], start=False, stop=(r_ == 3)),
                             r=["CTs%d" % pa, "prevbf%d" % pa], w=[PB(4 + pa)])
                if c < NT - 1:
                    S.op("dve", lambda e: e.tensor_tensor(
                        out=prev32[:, :, :], in0=prev32[:, :, :],
                        in1=cdA[:, c, g * 4:(g + 1) * 4].unsqueeze(2).broadcast_to([128, 4, 64]), op=ALU.mult),
                        r=["prev32", "cdA"], w=["prev32"])
                    S.op("dve", lambda e: e.tensor_tensor(
                        out=prev32[:, :, :], in0=prev32[:, :, :],
                        in1=bank(7)[:, pa * 256:(pa + 1) * 256].rearrange("p (r q) -> p r q", r=4), op=ALU.add),
                        r=["prev32", PB(7)], w=["prev32"])
                    S.op("act", lambda e: e.activation(out=prevbf[1 - pa][:, :, :], in_=prev32[:, :, :], func=AF.Copy),
                         r=["prev32"], w=["prevbf%d" % (1 - pa)])

            def stV(c, g=g):
                pa = c % 2
                yb = bank(4 + pa)
                S.op("dve", lambda e: e.scalar_tensor_tensor(out=vall[:, c - MT0, :], in0=yb[:, 0:256], scalar=0.5,
                                                            in1=szz[pa][:, :], op0=ALU.mult, op1=ALU.mult),
                     r=[PB(4 + pa), "szz%d" % pa], w=["vall%d" % c])
                S.op("act", lambda e: e.activation(out=junk2[:, :], in_=vall[:, c - MT0, :], func=AF.Square,
                                                   accum_out=gsA[:, c - MT0:c - MT0 + 1]),
                     r=["vall%d" % c], w=["junk2", "gsA%d" % c])

            def ismain(c):
                return MT0 <= c < NT

            for step in range(MT0 - 3, NT + 1):
                if ismain(step + 3):
                    stA(step + 3)
                if ismain(step + 2):
                    stBC(step + 2)
                if ismain(step - 1):
                    stV(step - 1)
                if ismain(step + 1):
                    stDE(step + 1)
                if MT0 <= step + 1 < NT - 1:
                    stS(step + 1)
                if MT0 <= step < NT:
                    stF(step)
            S.op("dve", lambda e: e.tensor_scalar(out=gsA[:, 8:16], in0=gsA[:, 0:8], scalar1=1.0 / 256.0, scalar2=EPS,
                                                 op0=ALU.mult, op1=ALU.add),
                 r=["gsA%d" % c for c in range(MT0, NT)], w=["gsB"])
            S.op("act", lambda e: e.activation(out=gsA[:, 16:24], in_=gsA[:, 8:16], func=AF.Sqrt), r=["gsB"], w=["gsC"])
            S.op("dve", lambda e: e.reciprocal(out=gsA[:, 24:32], in_=gsA[:, 16:24]), r=["gsC"], w=["gsD"])
            for c in range(MT0, NT):
                q4 = (c - MT0) % 4
                tb = (4, 5, 2, 3)[q4]
                S.op("dve", lambda e, c=c, q4=q4: e.scalar_tensor_tensor(
                    out=vn4[q4][:, :], in0=vall[:, c - MT0, :], scalar=gsA[:, 24 + c - MT0:25 + c - MT0],
                    in1=nwg[:, :], op0=ALU.mult, op1=ALU.mult), r=["vall%d" % c, "gsD", "nwg"], w=["vn%d" % q4])
                for i in range(2):
                    S.op("pe", lambda e, i=i, q4=q4, tb=tb: e.transpose(
                        out=bankbf(tb)[:, 512 + i * 128:512 + (i + 1) * 128], in_=vn4[q4][:, i * 128:(i + 1) * 128],
                        identity=ident[:, :]), r=["vn%d" % q4] + CONST, w=[PB(tb)])
                S.op("act", lambda e, c=c, g=g, tb=tb: e.activation(
                    out=yT[:, 16 + 2 * g:18 + 2 * g, (c - MT0) * 128:(c - MT0 + 1) * 128],
                    in_=bankbf(tb)[:, 512:768].rearrange("p (i c) -> p i c", i=2), func=AF.Copy),
                    r=[PB(tb)], w=["yTs"])
        if dbg and stage == 2:
            dump("yTs", yT[:, 16:32, :], [128, 16, 1024], ["yTs"], BF16)

    if stage >= 3:
        for j_ in range(2):
            wload(j_, wcv_d[j_], wsl[j_][:, :].rearrange("p (k c) -> p k c", k=KT), key=("cv", 0, j_))
        S.barrier()
        o = T0
        convout, _, n_ = sb("convout", [128, 16, 512], F32, at=o); o += n_
        ubuf, _, n_ = sb("ubuf", [128, 544], BF16, at=o); o += n_
        sgm, _, n_ = sb("sgm", [128, 512], F32, at=o); o += n_
        sqb, _, n_ = sb("sqb", [128, 512], BF16, at=o); o += n_
        cbf, _, n_ = sb("cbf", [128, 512], BF16, at=o); o += n_
        assert o <= T0 + 40960, o
        st_mean, _, n_ = sb("st_mean", [128, 512], F32, at=o); o += n_
        ubuf2 = nc.alloc_sbuf_tensor_at("ubuf2", [128, 544], BF16, offset=HN0_OFF[0])
        sgm2 = nc.alloc_sbuf_tensor_at("sgm2", [128, 512], F32, offset=HN0_OFF[0] + 1088)
        ubufs = [ubuf, ubuf2]
        sgms = [sgm, sgm2]
        assert o <= T0 + 40960, o
        dg31 = nc.alloc_sbuf_tensor_at("dg31", [128, 31, 128], BF16, offset=CEND_W2[0])
        hm0 = MT0 * 128 - PAD
        def ln(j, mb):
            S.op("dve", lambda e: e.tensor_tensor(out=convout[:, j, :], in0=convout[:, j, :],
                                                 in1=st_mean[:, :], op=ALU.subtract),
                 r=["convout%d" % j, "st_mean"], w=["convout%d" % j])
            S.op("dve", lambda e: e.tensor_tensor(out=convout[:, j, :], in0=convout[:, j, :], in1=sgm[:, :],
                                                 op=ALU.mult), r=["convout%d" % j, "sgm0"], w=["convout%d" % j])
            S.op("act", lambda e: e.activation(out=yT[:, j, mb * 512:(mb + 1) * 512],
                                               in_=convout[:, j, :], func=AF.Silu,
                                               bias=lnb[:, j:j + 1], scale=lng[:, j:j + 1]),
                 r=["convout%d" % j] + CONST, w=["yTc%d" % j])

        for mb in range(2):
            m0 = hm0 + mb * 512
            def glu(j, m0=m0, mb=mb):
                sl_ = j % 2
                wv = wsl[sl_][:, :].rearrange("p (k c) -> p k c", k=KT)
                wload(sl_, wcv_d[j], wv, key=("cv", mb, j))
                b0, b1 = (0, 1) if j % 2 == 0 else (6, 7)
                ub = ubufs[j % 2]
                sgi = (j % 2) if mb == 0 else 1
                sg = sgms[sgi]
                pj = j % 2
                specs = ((m0 - 32, 32, 0, bank(5)[:, pj * 64:pj * 64 + 32], bank(5)[:, pj * 64 + 32:pj * 64 + 64], 5, 5),
                         (m0, 512, 32, bank(b0)[:, :], bank(b1)[:, :], b0, b1))
                for (t0, n, uo, pv, pg, nb0, nb1) in specs:
                    for half, pp_, nb in ((0, pv, nb0), (1, pg, nb1)):
                        for k in range(KT):
                            S.op("pe", lambda e, k=k, t0=t0, n=n, half=half, pp_=pp_: e.matmul(
                                pp_, lhsT=wv[:, k, half * 128:(half + 1) * 128],
                                rhs=hnT[:, k, t0:t0 + n], start=(k == 0), stop=(k == KT - 1)),
                                r=["wsl%d" % sl_] + hn_res(MT0 - 1, NT), w=[PB(nb)])
                for (t0, n, uo, pv, pg, nb0, nb1) in specs:
                    S.op("act", lambda e, n=n, pg=pg: e.activation(out=sg[:, 0:n], in_=pg, func=AF.Sigmoid),
                         r=[PB(nb1)], w=["sgm%d" % sgi])
                    S.op("dve", lambda e, n=n, uo=uo, pv=pv: e.tensor_tensor(out=ub[:, uo:uo + n], in0=pv,
                                                                            in1=sg[:, 0:n], op=ALU.mult),
                         r=[PB(nb0), "sgm%d" % sgi], w=["ubuf%d" % (j % 2)])

            def dbuild(j):
                for kk in range(31):
                    col = cw31[:, j * 31 + kk:j * 31 + kk + 1]
                    if kk < 16:
                        S.op("act", lambda e, kk=kk, col=col: e.activation(out=dg31[:, kk, :], in_=ident[:, :],
                                                                          func=AF.Copy, scale=col),
                             r=CONST, w=["dg31a"])
                    else:
                        S.op("dve", lambda e, kk=kk, col=col: e.tensor_scalar(
                            out=dg31[:, kk, :], in0=ident[:, :], scalar1=col, scalar2=None, op0=ALU.mult),
                            r=CONST, w=["dg31b"])

            def conv(j):
                ub = ubufs[j % 2]
                for kk in range(31):
                    S.op("pe", lambda e, kk=kk: e.matmul(bank(2)[:, :], lhsT=dg31[:, kk, :],
                                                        rhs=ub[:, 2 + kk:2 + kk + 512], start=(kk == 0),
                                                        stop=(kk == 30)),
                         r=["dg31a" if kk < 16 else "dg31b", "ubuf%d" % (j % 2)], w=[PB(2)])
                S.op("act", lambda e: e.activation(out=convout[:, j, :], in_=bank(2)[:, :], func=AF.Identity,
                                                   bias=cb31[:, j:j + 1]), r=[PB(2)] + CONST, w=["convout%d" % j])
                S.op("act", lambda e: e.activation(out=sqb[:, :], in_=bank(2)[:, :], func=AF.Square,
                                                   bias=cb31[:, j:j + 1]), r=[PB(2)] + CONST, w=["sqb"])
                S.op("dve", lambda e: e.tensor_copy(out=cbf[:, :], in_=convout[:, j, :]),
                     r=["convout%d" % j], w=["cbf"])
                S.op("pe", lambda e: e.matmul(bank(3)[:, :], lhsT=onesb[:, :], rhs=cbf[:, :], start=(j == 0),
                                              stop=(j == 15)), r=["cbf"] + CONST, w=[PB(3)])
                S.op("pe", lambda e: e.matmul(bank(4)[:, :], lhsT=onesb[:, :], rhs=sqb[:, :], start=(j == 0),
                                              stop=(j == 15)), r=["sqb"] + CONST, w=[PB(4)])

            dbuild(0)
            glu(0)
            if mb == 1:
                ln(0, 0)
            for j in range(16):
                if j + 1 < 16:
                    glu(j + 1)
                conv(j)
                if mb == 1 and j + 1 < 16:
                    ln(j + 1, 0)
                if j + 1 < 16:
                    dbuild(j + 1)
            S.op("dve", lambda e: e.tensor_scalar(out=st_mean[:, :], in0=bank(3)[:, :], scalar1=1.0 / 2048.0,
                                                 scalar2=None, op0=ALU.mult), r=[PB(3)], w=["st_mean"])
            S.op("dve", lambda e: e.tensor_tensor(out=sgm[:, :], in0=st_mean[:, :], in1=st_mean[:, :], op=ALU.mult),
                 r=["st_mean"], w=["sgm0"])
            S.op("dve", lambda e: e.scalar_tensor_tensor(out=sgm[:, :], in0=bank(4)[:, :], scalar=1.0 / 2048.0,
                                                        in1=sgm[:, :], op0=ALU.mult, op1=ALU.subtract),
                 r=[PB(4), "sgm0"], w=["sgm0"])
            S.op("dve", lambda e: e.tensor_scalar(out=sgm[:, :], in0=sgm[:, :], scalar1=EPS, scalar2=None,
                                                 op0=ALU.add), r=["sgm0"], w=["sgm0"])
            S.op("act", lambda e: e.activation(out=sgm[:, :], in_=sgm[:, :], func=AF.Sqrt), r=["sgm0"], w=["sgm0"])
            S.op("dve", lambda e: e.reciprocal(out=sgm[:, :], in_=sgm[:, :]), r=["sgm0"], w=["sgm0"])
        for j_ in range(2):
            wload(j_, wcs_d[j_], wsl[j_][:, 0:2048].rearrange("p (k c) -> p k c", k=KT), key=("cs", j_))
        S.barrier()
        gate = [sgm2, sgm2]
        for j in range(16):
            sl_ = j % 3
            wv = wsl[sl_][:, 0:2048].rearrange("p (k c) -> p k c", k=KT)
            wload(sl_, wcs_d[j], wv, key=("cs", j))
            ln(j, 1)
            for mb in range(2):
                bb = (5, 0)[mb]
                for k in range(KT):
                    S.op("pe", lambda e, k=k, wv=wv, mb=mb, bb=bb: e.matmul(
                        bank(bb)[:, :], lhsT=wv[:, k, :], rhs=hnT[:, k, hm0 + mb * 512:hm0 + (mb + 1) * 512],
                        start=(k == 0), stop=(k == KT - 1)), r=["wsl%d" % sl_] + hn_res(MT0, NT), w=[PB(bb)])
                S.op("act", lambda e, mb=mb, bb=bb: e.activation(out=gate[mb][:, :], in_=bank(bb)[:, :], func=AF.Silu),
                     r=[PB(bb)], w=["sgm1"])
                S.op("dve", lambda e, j=j, mb=mb: e.tensor_tensor(out=yT[:, j, mb * 512:(mb + 1) * 512],
                                                                 in0=yT[:, j, mb * 512:(mb + 1) * 512],
                                                                 in1=gate[mb][:, :], op=ALU.mult),
                     r=["yTc%d" % j, "sgm1"], w=["yTc%d" % j])
        if dbg and stage == 3:
            dump("yTc", yT[:, 0:16, :], [128, 16, 1024], ["yTc%d" % j for j in range(16)], BF16)

    if stage >= 4:
        for c_ in range(3):
            wload(c_, wout_d[c_], wsl[c_][:, :].rearrange("p (k c) -> p k c", k=32), key=("wo", c_))
        S.barrier()
        hout = nc.alloc_sbuf_tensor_at("hout", [128, NMAIN, D], F32, offset=BASE)
        fnw, _, _ = sb("fnwS", [128, D], F32, at=T0)
        junk3, _, _ = sb("junk3", [128, D], BF16, at=T0 + 8192)
        osml, _, _ = sb("osml", [128, 32], F32, at=T0 + 12288)
        S.op("sp", lambda e: e.dma_start(out=fnw[:, :], in_=fnw_d), w=["fnw"], dma="fnw")
        for t in range(NMAIN):
            S.op("sp", lambda e, t=t: e.dma_start(out=hout[:, t, :],
                                                  in_=xin[(MT0 + t) * 128:(MT0 + t + 1) * 128, :]),
                 w=["hout%d" % t], dma="hres%d" % t)
        for cbk in range(16):
            sl_ = cbk % 3
            wv = wsl[sl_][:, :].rearrange("p (k c) -> p k c", k=32)
            wload(sl_, wout_d[cbk], wv, key=("wo", cbk))
            for t in range(NMAIN):
                pb = (cbk * NMAIN + t) % 4
                for k in range(32):
                    S.op("pe", lambda e, k=k, t=t, pb=pb, wv=wv: e.matmul(
                        bank(pb)[:, 0:128], lhsT=yT[:, k, t * 128:(t + 1) * 128], rhs=wv[:, k, :],
                        start=(k == 0), stop=(k == 31)), r=["wsl%d" % sl_, "yTs"] + ["yTc%d" % j for j in range(16)], w=[PB(pb)])
                S.op("dve", lambda e, t=t, pb=pb, cbk=cbk: e.tensor_tensor(
                    out=hout[:, t, cbk * 128:(cbk + 1) * 128], in0=bank(pb)[:, 0:128],
                    in1=hout[:, t, cbk * 128:(cbk + 1) * 128], op=ALU.add), r=[PB(pb), "hout%d" % t], w=["hout%d" % t])
        for t in range(NMAIN):
            S.op("act", lambda e, t=t: e.activation(out=junk3[:, :], in_=hout[:, t, :], func=AF.Square,
                                                   accum_out=osml[:, 4 * t:4 * t + 1]),
                 r=["hout%d" % t], w=["junk3", "oA%d" % t])
            S.op("dve", lambda e, t=t: e.tensor_scalar(out=osml[:, 4 * t + 1:4 * t + 2], in0=osml[:, 4 * t:4 * t + 1],
                                                      scalar1=1.0 / D, scalar2=EPS, op0=ALU.mult, op1=ALU.add),
                 r=["oA%d" % t], w=["oB%d" % t])
            S.op("act", lambda e, t=t: e.activation(out=osml[:, 4 * t + 2:4 * t + 3], in_=osml[:, 4 * t + 1:4 * t + 2],
                                                   func=AF.Sqrt), r=["oB%d" % t], w=["oC%d" % t])
            S.op("dve", lambda e, t=t: e.reciprocal(out=osml[:, 4 * t + 3:4 * t + 4], in_=osml[:, 4 * t + 2:4 * t + 3]),
                 r=["oC%d" % t], w=["oD%d" % t])
            S.op("dve", lambda e, t=t: e.scalar_tensor_tensor(out=hout[:, t, :], in0=hout[:, t, :],
                                                             scalar=osml[:, 4 * t + 3:4 * t + 4], in1=fnw[:, :],
                                                             op0=ALU.mult, op1=ALU.mult),
                 r=["hout%d" % t, "oD%d" % t, "fnw"], w=["hout%d" % t])
            S.op("sp", lambda e, t=t: e.dma_start(out=out_d[t * 128:(t + 1) * 128, :], in_=hout[:, t, :]),
                 r=["hout%d" % t], w=["outd%d" % t], dma="ost")
    S.barrier()
    S.emit(nc)
    return nc, dbg_outs


def prep_inputs(inp):
    f32 = np.float32
    x = np.asarray(inp["x"], f32)
    meta = np.asarray(inp["meta_tokens"], f32)
    w_in = np.asarray(inp["w_in"], f32)[0]
    C = 2048

    def pk(v, n):
        return np.ascontiguousarray(np.asarray(v, f32).reshape(n, 128).T)

    def bc(v):
        return np.ascontiguousarray(np.broadcast_to(np.asarray(v, f32).reshape(1, -1), (128, np.asarray(v).size)))

    shared = {}
    shared["normw"] = pk(inp["norm_w"][0], 16)
    def blk(wcols):
        c = wcols.shape[1]
        return np.ascontiguousarray(wcols.reshape(16, 128, c).transpose(1, 0, 2))

    shared["w_dt"] = blk(w_in[:, 12288:12320])
    zc, xc, Bc, Cc = 3 * C, 4 * C, 5 * C, 5 * C + 1024
    wssd = np.empty((8, 3, 128, 16, 256), f32)
    for g in range(8):
        wssd[g, 0] = blk(w_in[:, xc + g * 256: xc + (g + 1) * 256])
        wssd[g, 1, :, :, 0:128] = blk(w_in[:, Bc + g * 128: Bc + (g + 1) * 128])
        wssd[g, 1, :, :, 128:256] = blk(w_in[:, Cc + g * 128: Cc + (g + 1) * 128])
        wssd[g, 2] = blk(w_in[:, zc + g * 256: zc + (g + 1) * 256])
    shared["w_ssd"] = wssd
    wcv = np.empty((16, 128, 16, 256), f32)
    wcs = np.empty((16, 128, 16, 128), f32)
    for j in range(16):
        wcv[j, :, :, 0:128] = blk(w_in[:, j * 128:(j + 1) * 128])
        wcv[j, :, :, 128:256] = blk(w_in[:, C + j * 128: C + (j + 1) * 128])
        wcs[j] = blk(w_in[:, 2 * C + j * 128: 2 * C + (j + 1) * 128])
    shared["w_cv"] = wcv
    shared["w_cs"] = wcs
    wo = np.asarray(inp["w_out"], f32)[0]
    shared["w_out"] = np.ascontiguousarray(wo.reshape(32, 128, 16, 128).transpose(2, 1, 0, 3))
    scw = np.asarray(inp["ssd_conv_w"], f32)[0]
    scb = np.asarray(inp["ssd_conv_b"], f32)[0]
    cw4 = np.empty((128, 8, 4, 4), f32)
    cb4 = np.empty((128, 8, 4), f32)
    for g in range(8):
        chs = [g * 256, g * 256 + 128, 2048 + g * 128, 3072 + g * 128]
        for ct in range(4):
            cw4[:, g, ct, :] = scw[:, chs[ct]:chs[ct] + 128].T
            cb4[:, g, ct] = scb[chs[ct]:chs[ct] + 128]
    shared["cw4"] = cw4.reshape(128, 128)
    shared["cb4"] = cb4.reshape(128, 32)
    c31 = np.asarray(inp["conf_dw_w"], f32)[0]
    shared["cw31"] = np.ascontiguousarray(c31.reshape(31, 16, 128).transpose(2, 1, 0)).reshape(128, 16 * 31)
    shared["cb31"] = pk(inp["conf_dw_b"][0], 16)
    shared["lng"] = pk(inp["conf_ln_g"][0], 16)
    shared["lnb"] = pk(inp["conf_ln_b"][0], 16)
    shared["dtb"] = bc(inp["dt_bias"][0])
    shared["alog"] = bc(inp["A_log"][0])
    shared["dsk"] = bc(inp["D_skip"][0])
    shared["snw"] = bc(inp["ssd_norm_w"][0])
    shared["fnw"] = bc(inp["final_norm_w"])
    bf = ml_dtypes.bfloat16
    shared["ident"] = np.eye(128, dtype=f32).astype(bf)
    shared["onesb"] = np.ones((128, 128), f32).astype(bf)
    shared["onesf"] = np.ones((128, 128), f32)
    shared["U"] = np.triu(np.ones((128, 128), f32))
    shared["m01"] = np.triu(np.ones((128, 128), f32)).astype(bf)
    in_maps = []
    for core in range(8):
        b, half = core // 2, core % 2
        xi = np.zeros((TP, D), f32)
        mk = np.zeros((TP,), f32)
        if half == 0:
            xi[1136:1152] = meta
            xi[1152:2176] = x[b, 0:1024]
            mk[1136:] = 1.0
        else:
            xi[112:128] = meta
            xi[128:2176] = x[b, 0:2048]
            mk[112:] = 1.0
        m = dict(shared)
        m["xin"] = xi
        m["mask"] = np.ascontiguousarray(mk.reshape(NT, 128).T)
        in_maps.append(m)
    return in_maps


_NC_CACHE = {}


def kernel(**inputs):
    in_maps = prep_inputs(inputs)
    if "nc" not in _NC_CACHE:
        _NC_CACHE["nc"] = build()[0]
    nc = _NC_CACHE["nc"]
    res = run_bass_kernel_spmd(nc, in_maps, core_ids=list(range(8)))
    out = np.empty((4, 2048, D), np.float32)
    for core in range(8):
        b, half = core // 2, core % 2
        out[b, half * 1024:(half + 1) * 1024] = np.asarray(res.results[core]["out"], np.float32)
    return out
```

```python
import numpy as np
import ml_dtypes
import concourse.bass as bass
import concourse.mybir as mybir
from concourse.bass_utils import run_bass_kernel_spmd

F32 = mybir.dt.float32
BF16 = mybir.dt.bfloat16
AF = mybir.ActivationFunctionType
ALU = mybir.AluOpType

D = 2048
KT = 16
TP = 2176
NTOK = 2064
PAD = 112
NT = 17
MT0 = 9
NMAIN = 8
EPS = 1e-5
PCE = "dve"
NG = 8


class Sched:
    ENGS = ("pe", "act", "dve", "pool", "sp")

    def __init__(self):
        self.ops = {e: [] for e in self.ENGS}
        self.last_w = {}
        self.readers = {}
        self.dma_cnt = {}

    def op(self, eng, fn, r=(), w=(), dma=None):
        def _n(x):
            return x[:3] if x.startswith("ps") else x
        w = [_n(x) for x in w] + [_n(x) for x in r if x.startswith("ps")]
        r = [x for x in r if not x.startswith("ps")]
        idx = len(self.ops[eng])
        deps = set()
        for x in r:
            if x in self.last_w:
                deps.add(self.last_w[x])
        for x in w:
            if x in self.last_w:
                deps.add(self.last_w[x])
            rd = self.readers.get(x)
            if rd:
                deps.update(rd.values())
        if dma is not None:
            self.dma_cnt[dma] = self.dma_cnt.get(dma, 0) + 1
            tok = ("dma", dma, 16 * self.dma_cnt[dma])
        else:
            tok = ("eng", eng, idx)
        if eng == "pe" and dma is None:
            deps = {d for d in deps if not (d[0] == "eng" and d[1] == "pe")}
        for x in w:
            self.last_w[x] = tok
            self.readers[x] = {}
        for x in r:
            key = eng if dma is None else ("dma", dma, self.dma_cnt[dma])
            self.readers.setdefault(x, {})[key] = tok
        self.ops[eng].append(dict(fn=fn, deps=deps, tok=tok, dma=dma))
        return tok

    def barrier(self):
        toks = set()
        for e in self.ENGS:
            for o in reversed(self.ops[e]):
                if o["dma"] is None:
                    toks.add(o["tok"])
                    break
        for ch, c in self.dma_cnt.items():
            toks.add(("dma", ch, 16 * c))
        for e in self.ENGS:
            self.ops[e].append(dict(fn=None, deps=set(toks), tok=("eng", e, len(self.ops[e])), dma=None))

    def emit(self, nc):
        targets = {e: set() for e in self.ENGS}
        for e in self.ENGS:
            for o in self.ops[e]:
                for d in o["deps"]:
                    if d[0] == "eng":
                        targets[d[1]].add(d[2])
        counts = {}
        for e in self.ENGS:
            c = 0
            cl = []
            for i, o in enumerate(self.ops[e]):
                if i in targets[e] and o["dma"] is None:
                    c += 1
                cl.append(c)
            counts[e] = cl
        import contextlib
        with contextlib.ExitStack() as st:
            esem = {e: st.enter_context(nc.semaphore("s_" + e)) for e in self.ENGS}
            dsem = {ch: st.enter_context(nc.semaphore("d_" + str(ch))) for ch in self.dma_cnt}
            block = st.enter_context(nc.Block())

            def run(e, h):
                seen = {}
                for i, o in enumerate(self.ops[e]):
                    need = {}
                    for d in o["deps"]:
                        if d[0] == "eng":
                            sem, val, key = esem[d[1]], counts[d[1]][d[2]], ("e", d[1])
                        else:
                            sem, val, key = dsem[d[1]], d[2], ("d", d[1])
                        if key not in need or need[key][1] < val:
                            need[key] = (sem, val)
                    for key in sorted(need, key=str):
                        sem, val = need[key]
                        if seen.get(key, 0) >= val:
                            continue
                        seen[key] = val
                        h.wait_ge(sem, val)
                    if o["fn"] is None:
                        ins = None
                        if i in targets[e]:
                            ins = h.nop()
                    else:
                        ins = o["fn"](h)
                    if o["dma"] is not None:
                        ins.then_inc(dsem[o["dma"]], 16)
                    elif i in targets[e]:
                        ins.then_inc(esem[e], 1)

            @block.tensor
            def _(h):
                run("pe", h)

            @block.scalar
            def _(h):
                run("act", h)

            @block.vector
            def _(h):
                run("dve", h)

            @block.gpsimd
            def _(h):
                run("pool", h)

            @block.sync
            def _(h):
                run("sp", h)


def build(stage=99, dbg=False):
    nc = bass.Bass("TRN2", target_bir_lowering=False)
    S = Sched()

    def din(name, shape, dt=F32):
        return nc.dram_tensor(name, list(shape), dt, kind="ExternalInput").ap()

    xin = din("xin", [TP, D])
    maskd = din("mask", [128, NT])
    normw_d = din("normw", [128, KT])
    wdt_d = din("w_dt", [128, KT, 32])
    wssd_d = din("w_ssd", [NG, 3, 128, KT, 256])
    wcv_d = din("w_cv", [16, 128, KT, 256])
    wcs_d = din("w_cs", [16, 128, KT, 128])
    wout_d = din("w_out", [16, 128, 32, 128])
    cw4_d = din("cw4", [128, NG * 16])
    cb4_d = din("cb4", [128, NG * 4])
    cw31_d = din("cw31", [128, 16 * 31])
    cb31_d = din("cb31", [128, 16])
    lng_d = din("lng", [128, 16])
    lnb_d = din("lnb", [128, 16])
    dtb_d = din("dtb", [128, 32])
    alog_d = din("alog", [128, 32])
    dsk_d = din("dsk", [128, 32])
    snw_d = din("snw", [128, D])
    fnw_d = din("fnw", [128, D])
    ident_d = din("ident", [128, 128], BF16)
    onesb_d = din("onesb", [128, 128], BF16)
    m01_d = din("m01", [128, 128], BF16)
    onesf_d = din("onesf", [128, 128])
    U_d = din("U", [128, 128])
    out_d = nc.dram_tensor("out", [1024, D], F32, kind="ExternalOutput").ap()
    dbg_outs = {}

    BASE = 16512
    off = [BASE]

    def sb(name, shape, dt, at=None):
        esz = 4 if dt == F32 else 2
        n = 1
        for s in shape[1:]:
            n *= s
        nbytes = (n * esz + 31) // 32 * 32
        if at is None:
            o = off[0]
            off[0] += nbytes
        else:
            o = at
        t = nc.alloc_sbuf_tensor_at(name, list(shape), dt, offset=o)
        return t, o, nbytes

    hnT, _, _ = sb("hnT", [128, KT, NTOK], BF16)
    hn0, _hn0o, _ = sb("hn0", [128, KT, 128], BF16)
    HN0_OFF = [_hn0o]
    T0 = off[0]
    off[0] += 40960
    Y0 = off[0]
    yT, _, _ = sb("yT", [128, 32, 1024], BF16)
    _w = [sb("wsl%d" % i, [128, 4096], BF16) for i in range(3)]
    wsl = [w_[0] for w_ in _w]
    CEND_W2 = [_w[2][1]]
    ident, _, _ = sb("identS", [128, 128], BF16)
    onesb, _, _ = sb("onesbS", [128, 128], BF16)
    m01, _, _ = sb("m01S", [128, 128], BF16)
    onesf, _, _ = sb("onesfS", [128, 128], F32)
    Um, _, _ = sb("US", [128, 128], F32)
    normw, _, _ = sb("normwS", [128, KT], F32)
    maskS, _, _ = sb("maskS", [128, NT], F32)
    cw4, _, _ = sb("cw4S", [128, NG * 16], F32)
    cb4, _, _ = sb("cb4S", [128, NG * 4], F32)
    cw31, _, _ = sb("cw31S", [128, 16 * 31], F32)
    cb31, _, _ = sb("cb31S", [128, 16], F32)
    lng, _, _ = sb("lngS", [128, 16], F32)
    lnb, _, _ = sb("lnbS", [128, 16], F32)
    dtb, _, _ = sb("dtbS", [128, 32], F32)
    Aneg, _, _ = sb("AnegS", [128, 32], F32)
    dsk, _, _ = sb("dskS", [128, 32], F32)
    wdt, _, _ = sb("wdtS", [128, KT, 32], BF16)
    dtA, _, _ = sb("dtA", [128, NT, 32], F32)
    dAA, _, _ = sb("dAA", [128, NT, 32], F32)
    sml, _, _ = sb("sml", [128, 64], F32)
    CEND = off[0]
    assert CEND <= 229344, CEND

    PP = [nc.alloc_psum_tensor("pp%d" % i, [128, 1024], F32) for i in range(4)]

    def bank(b):
        return PP[b // 2][:, (b % 2) * 512:(b % 2) * 512 + 512]

    def bankbf(b):
        return PP[b // 2][:, (b % 2) * 512:(b % 2) * 512 + 512].bitcast(BF16)

    def PB(b):
        return "ps%d" % b

    cl = [(ident, ident_d), (onesb, onesb_d), (m01, m01_d), (onesf, onesf_d), (Um, U_d), (normw, normw_d),
          (maskS, maskd), (cw4, cw4_d), (cb4, cb4_d), (cw31, cw31_d), (cb31, cb31_d), (lng, lng_d),
          (lnb, lnb_d), (dtb, dtb_d), (Aneg, alog_d), (dsk, dsk_d)]
    for i, (s_, d_) in enumerate(cl):
        S.op("sp", lambda e, s_=s_, d_=d_: e.dma_start(out=s_[:, :], in_=d_), w=["const%d" % i], dma="const")
    CONST = ["const%d" % i for i in range(len(cl))]
    S.op("pool", lambda e: e.dma_start(out=wdt[:, :, :], in_=wdt_d),
         w=["wdt"], dma="wdt")
    S.op("act", lambda e: e.activation(out=Aneg[:, :], in_=Aneg[:, :], func=AF.Exp), r=CONST, w=["Aneg"])
    S.op("dve", lambda e: e.tensor_scalar(out=Aneg[:, :], in0=Aneg[:, :], scalar1=-1.0, scalar2=None,
                                          op0=ALU.mult), r=["Aneg"], w=["Aneg"])

    xst = [sb("xst%d" % i, [128, D], F32, at=T0 + (i * 8192 if i < 2 else 29184))[0] for i in range(3)]
    xs = [sb("xs%d" % i, [128, D], BF16, at=T0 + 16384 + i * 4096)[0] for i in range(2)]
    junk, _, _ = sb("junk", [128, D], BF16, at=T0 + 24576)
    ssA, _, _ = sb("ssA", [128, NT * 4], F32, at=T0 + 28672)
    def phA1(t):
        sl = t % 3
        sl2 = t % 2
        S.op("sp", lambda e, t=t, sl=sl, sl2=sl2: e.dma_start(out=xst[sl][:, 0:1024], in_=xin[t * 128:(t + 1) * 128, 0:1024]),
             w=["xstA%d" % sl], dma="x%d" % sl)
        S.op("act", lambda e, t=t, sl=sl, sl2=sl2: e.dma_start(out=xst[sl][:, 1024:2048], in_=xin[t * 128:(t + 1) * 128, 1024:2048]),
             w=["xstB%d" % sl], dma="xb%d" % sl)
        S.op("act", lambda e, t=t, sl=sl, sl2=sl2: e.activation(out=junk[:, :], in_=xst[sl][:, :], func=AF.Square,
                                                      accum_out=ssA[:, 4 * t:4 * t + 1]),
             r=["xstA%d" % sl, "xstB%d" % sl], w=["junk", "ssA%d" % t])
        S.op("dve", lambda e, t=t: e.tensor_scalar(out=ssA[:, 4 * t + 1:4 * t + 2], in0=ssA[:, 4 * t:4 * t + 1],
                                                  scalar1=1.0 / D, scalar2=EPS, op0=ALU.mult, op1=ALU.add),
             r=["ssA%d" % t], w=["ssB%d" % t])
    def phA2(t):
        sl = t % 3
        sl2 = t % 2
        S.op("act", lambda e, t=t: e.activation(out=ssA[:, 4 * t + 2:4 * t + 3], in_=ssA[:, 4 * t + 1:4 * t + 2],
                                               func=AF.Sqrt), r=["ssB%d" % t], w=["ssC%d" % t])
        S.op("dve", lambda e, t=t: e.reciprocal(out=ssA[:, 4 * t + 3:4 * t + 4], in_=ssA[:, 4 * t + 2:4 * t + 3]),
             r=["ssC%d" % t], w=["ssD%d" % t])
        S.op("act", lambda e, t=t, sl=sl, sl2=sl2: e.activation(out=xs[sl2][:, :], in_=xst[sl][:, :], func=AF.Copy,
                                                      scale=ssA[:, 4 * t + 3:4 * t + 4]),
             r=["xstA%d" % sl, "xstB%d" % sl, "ssD%d" % t], w=["xs%d" % sl2])
        b0 = 2 * sl2
        for k in range(KT):
            bb = b0 + (k // 8)
            S.op("pe", lambda e, k=k, bb=bb, sl=sl, sl2=sl2: e.transpose(out=bankbf(bb)[:, (k % 8) * 128:(k % 8) * 128 + 128],
                                                              in_=xs[sl2][:, k * 128:(k + 1) * 128],
                                                              identity=ident[:, :]),
                 r=["xs%d" % sl2] + CONST, w=[PB(bb)])
        for hh in range(2):
            bb = b0 + hh
            src = bankbf(bb).rearrange("p (k c) -> p k c", k=8)
            nwb = normw[:, hh * 8:(hh + 1) * 8].unsqueeze(2)
            if t == 0:
                S.op("dve", lambda e, src=src, nwb=nwb, hh=hh: e.tensor_tensor(
                    out=hn0[:, hh * 8:(hh + 1) * 8, :], in0=src, in1=nwb.broadcast_to([128, 8, 128]), op=ALU.mult),
                    r=[PB(bb)] + CONST, w=["hn0"])
                S.op("dve", lambda e, src=src, nwb=nwb, hh=hh: e.tensor_tensor(
                    out=hnT[:, hh * 8:(hh + 1) * 8, 0:16], in0=src[:, :, PAD:128],
                    in1=nwb.broadcast_to([128, 8, 16]), op=ALU.mult),
                    r=[PB(bb)] + CONST, w=["hnT%d" % t])
            else:
                c0 = t * 128 - PAD
                S.op("dve", lambda e, src=src, nwb=nwb, hh=hh, c0=c0: e.tensor_tensor(
                    out=hnT[:, hh * 8:(hh + 1) * 8, c0:c0 + 128], in0=src,
                    in1=nwb.broadcast_to([128, 8, 128]), op=ALU.mult),
                    r=[PB(bb)] + CONST, w=["hnT%d" % t])


    phA1(0)
    for t in range(NT):
        if t + 1 < NT:
            phA1(t + 1)
        phA2(t)
    def hn_res(t0, t1):
        return ["hnT%d" % t for t in range(t0, t1)]

    def hcols(t):
        return None

    def lhs_tok(k, t):
        if t == 0:
            return hn0[:, k, :]
        return hnT[:, k, t * 128 - PAD:t * 128 - PAD + 128]

    S.barrier()
    for t in range(NT):
        for k in range(KT):
            S.op("pe", lambda e, k=k, t=t: e.matmul(bank(4)[:, 0:32], lhsT=lhs_tok(k, t), rhs=wdt[:, k, :],
                                                   start=(k == 0), stop=(k == KT - 1)),
                 r=["hnT%d" % t, "hn0", "wdt"], w=[PB(4)])
        S.op("dve", lambda e, t=t: e.tensor_tensor(out=dtA[:, t, :], in0=bank(4)[:, 0:32], in1=dtb[:, :], op=ALU.add),
             r=[PB(4)] + CONST, w=["dtA%d" % t])
        S.op("act", lambda e, t=t: e.activation(out=dtA[:, t, :], in_=dtA[:, t, :], func=AF.Exp),
             r=["dtA%d" % t], w=["dtA%d" % t])
        S.op("act", lambda e, t=t: e.activation(out=dtA[:, t, :], in_=dtA[:, t, :], func=AF.Ln, bias=1.0),
             r=["dtA%d" % t], w=["dtA%d" % t])
        S.op("dve", lambda e, t=t: e.tensor_scalar(out=dtA[:, t, :], in0=dtA[:, t, :], scalar1=maskS[:, t:t + 1],
                                                  scalar2=None, op0=ALU.mult),
             r=["dtA%d" % t] + CONST, w=["dtA%d" % t])
        S.op("dve", lambda e, t=t: e.tensor_tensor(out=dAA[:, t, :], in0=dtA[:, t, :], in1=Aneg[:, :], op=ALU.mult),
             r=["dtA%d" % t, "Aneg"], w=["dAA%d" % t])


    negAcA, _, _ = sb("negAcA", [128, NT, 32], F32, at=T0 + 33792)
    dsA, _, _ = sb("dsA", [128, NT, 32], F32, at=T0 + 33792 + 2176)
    cdA, _, _ = sb("cdA", [128, NT, 32], F32, at=T0 + 33792 + 4352)
    dAres = ["dAA%d" % t for t in range(NT)]
    for (c0, n) in ((0, 16), (16, 1)):
        src = dAA[:, c0:c0 + n, :].rearrange("p c h -> p (c h)")
        S.op("pe", lambda e, src=src, n=n: e.matmul(bank(5)[:, 0:n * 32], lhsT=Um[:, :], rhs=src, start=True, stop=True),
             r=dAres + CONST, w=[PB(5)])
        S.op("pe", lambda e, src=src, n=n: e.matmul(bank(6)[:, 0:n * 32], lhsT=onesf[:, :], rhs=src, start=True, stop=True),
             r=dAres + CONST, w=[PB(6)])
        dst = negAcA[:, c0:c0 + n, :].rearrange("p c h -> p (c h)")
        S.op("dve", lambda e, dst=dst, n=n: e.tensor_scalar(out=dst, in0=bank(5)[:, 0:n * 32], scalar1=-1.0, scalar2=None,
                                                           op0=ALU.mult), r=[PB(5)], w=["negAcA"])
        dd = dsA[:, c0:c0 + n, :].rearrange("p c h -> p (c h)")
        S.op("dve", lambda e, dd=dd, dst=dst, n=n: e.tensor_tensor(out=dd, in0=bank(6)[:, 0:n * 32], in1=dst, op=ALU.add),
             r=[PB(6), "negAcA"], w=["dsA"])
        if c0 == 0:
            S.op("dve", lambda e: e.memset(sml[:, 0:32], 0.0), w=["sml"])
            for c_ in range(MT0 - 1, -1, -1):
                if c_ < MT0 - 1:
                    S.op("dve", lambda e, c_=c_: e.tensor_tensor(out=dsA[:, c_, :], in0=dsA[:, c_, :], in1=sml[:, 0:32],
                                                                op=ALU.add), r=["dsA", "sml"], w=["dsA"])
                if c_ > 0:
                    S.op("dve", lambda e, c_=c_: e.tensor_tensor(out=sml[:, 0:32], in0=bank(6)[:, c_ * 32:(c_ + 1) * 32],
                                                                in1=sml[:, 0:32], op=ALU.add), r=[PB(6), "sml"], w=["sml"])
        S.op("act", lambda e, dd=dd: e.activation(out=dd, in_=dd, func=AF.Exp), r=["dsA"], w=["dsA"])
        cc_ = cdA[:, c0:c0 + n, :].rearrange("p c h -> p (c h)")
        S.op("act", lambda e, cc_=cc_, n=n: e.activation(out=cc_, in_=bank(6)[:, 0:n * 32], func=AF.Exp),
             r=[PB(6)], w=["cdA"])

    dumps = []

    def dump(name, ap, shape, res, dt=F32):
        d = nc.dram_tensor("dbg_" + name, list(shape), dt, kind="ExternalOutput").ap()
        dbg_outs[name] = d
        S.op("sp", lambda e, ap=ap, d=d: e.dma_start(out=d, in_=ap), r=res, w=["dbgout_" + name], dma="dbg_" + name)

    if dbg and stage <= 1:
        dump("hnT", hnT[:, :, :], [128, KT, NTOK], hn_res(0, NT), BF16)
        dump("dtA", dtA[:, :, :], [128, NT, 32], ["dtA%d" % t for t in range(NT)])
        dump("dAA", dAA[:, :, :], [128, NT, 32], ["dAA%d" % t for t in range(NT)])

    wchan = [0]

    preloaded = set()

    def wload(slot, src_ap, view, key=None):
        if key is not None:
            if key in preloaded:
                return
            preloaded.add(key)
        S.op("pool", lambda e, slot=slot, src_ap=src_ap, view=view: e.dma_start(out=view, in_=src_ap, max_dma_last_dim=8192),
             w=["wsl%d" % slot], dma="w%d" % slot)

    if stage >= 2:
        o = T0
        rawf, _, n_ = sb("rawf", [128, 3 + TP], BF16, at=o); o += n_
        xact, _, n_ = sb("xact", [128, TP], BF16, at=o); o += n_
        BT, _, n_ = sb("BT", [128, TP], BF16, at=o); o += n_
        CT, _, n_ = sb("CT", [128, TP], BF16, at=o); o += n_
        xtok, _, n_ = sb("xtok", [128, NT, 256], BF16, at=o); o += n_
        Btok, _, n_ = sb("Btok", [128, NT, 128], BF16, at=o); o += n_
        dg4, _, n_ = sb("dg4", [128, 4, 128], BF16, at=o); o += n_
        assert o <= T0 + 40960, o
        o = Y0
        def two(name, shape, dt):
            nonlocal_o = []
            return None
        bufs2 = {}
        def alloc2(name, shape, dt):
            nonlocal o
            lst = []
            for i_ in range(2):
                t_, _, n_ = sb(name + str(i_), shape, dt, at=o); o += n_
                lst.append(t_)
            return lst
        rhsA = alloc2("rhsA", [128, 4, 128], F32)
        segm = alloc2("segm", [128, 4, 128], F32)
        Lm = alloc2("Lm", [128, 4, 128], BF16)
        eAb = alloc2("eAb", [128, 4, 128], BF16)
        Mm = alloc2("Mm", [128, 4, 128], BF16)
        CTs = alloc2("CTs", [128, 4, 128], BF16)
        CBm = alloc2("CBm", [128, 128], BF16)
        xdd = alloc2("xdd", [128, 4, 64], BF16)
        prevbf = alloc2("prevbf", [128, 4, 64], BF16)
        szz = alloc2("szz", [128, 256], F32)
        vn = alloc2("vn", [128, 256], BF16)
        xDall, _, n_ = sb("xDall", [128, NMAIN, 256], BF16, at=o); o += n_
        vall, _, n_ = sb("vall", [128, NMAIN, 256], BF16, at=o); o += n_
        prev32, _, n_ = sb("prev32", [128, 4, 64], F32, at=o); o += n_
        gsA, _, n_ = sb("gsA", [128, 32], F32, at=o); o += n_
        nwg, _, n_ = sb("nwg", [128, 256], F32, at=o); o += n_
        assert o <= Y0 + 32768, o
        junk2, _, _ = sb("junk2", [128, 256], BF16, at=T0 + 31520)
        vn4 = [vn[0], vn[1], sb("vn2", [128, 256], BF16, at=T0 + 32032)[0], sb("vn3", [128, 256], BF16, at=T0 + 32544)[0]]

        S.op("dve", lambda e: e.memset(rawf[:, 0:3 + PAD], 0.0), w=["rawf_pad"])
        tblocks = [(0, 16)] + [(16 + i * 512, 512) for i in range(4)]
        pblocks = [(i * 512, 512) for i in range(4)] + [(2048, 128)]
        ipb = [0]
        for g in range(NG):
            wx = wsl[0][:, :].rearrange("p (k c) -> p k c", k=KT)
            wbc = wsl[1][:, :].rearrange("p (k c) -> p k c", k=KT)
            wz = wsl[2][:, :].rearrange("p (k c) -> p k c", k=KT)
            wload(0, wssd_d[g, 0], wx)
            wload(1, wssd_d[g, 1], wbc)
            wload(2, wssd_d[g, 2], wz)
            S.op("sp", lambda e, g=g: e.dma_start(out=nwg[:, :], in_=snw_d[:, g * 256:(g + 1) * 256]),
                 w=["nwg"], dma="nwg")
            for ct in range(4):
                wv = (wx, wx, wbc, wbc)[ct]
                wslot = (0, 0, 1, 1)[ct]
                cc = (0, 128, 0, 128)[ct]
                for kk in range(4):
                    S.op("dve", lambda e, g=g, ct=ct, kk=kk: e.tensor_scalar(
                        out=dg4[:, kk, :], in0=ident[:, :],
                        scalar1=cw4[:, g * 16 + ct * 4 + kk:g * 16 + ct * 4 + kk + 1], scalar2=None, op0=ALU.mult),
                        r=CONST, w=["dg4"])
                def segs(prefix, lo, hi):
                    return ["%s%d" % (prefix, i_) for i_ in range(max(lo, 0) // 512, (hi - 1) // 512 + 1)]
                if ct == 3:
                    tbl = [(1024, 16), (1040, 512), (1552, 512)]
                    pbl = [(1152, 512), (1664, 512)]
                else:
                    tbl, pbl = tblocks, pblocks
                for (t0, n) in tbl:
                    pb = ipb[0] % 2
                    ipb[0] += 1
                    for k in range(KT):
                        S.op("pe", lambda e, k=k, t0=t0, n=n, pb=pb, wv=wv, cc=cc: e.matmul(
                            bank(pb)[:, 0:n], lhsT=wv[:, k, cc:cc + 128], rhs=hnT[:, k, t0:t0 + n],
                            start=(k == 0), stop=(k == KT - 1)),
                            r=["wsl%d" % wslot] + hn_res(0, NT), w=[PB(pb)])
                    S.op("act", lambda e, t0=t0, n=n, pb=pb: e.activation(
                        out=rawf[:, 3 + PAD + t0:3 + PAD + t0 + n], in_=bank(pb)[:, 0:n], func=AF.Copy),
                        r=[PB(pb)], w=segs("rawf_s", t0 + PAD, t0 + PAD + n))
                dst = (xact, xact, BT, CT)[ct]
                dname = ("xact", "xact", "BT", "CT")[ct]
                for pi_, (p0, n) in enumerate(pbl):
                    cb_ = 2 + (pi_ % 2)
                    need = (["rawf_pad"] if p0 == 0 else []) + segs("rawf_s", p0 - 3, p0 + n)
                    for kk in range(4):
                        S.op("pe", lambda e, kk=kk, p0=p0, n=n, cb_=cb_: e.matmul(
                            bank(cb_)[:, 0:n], lhsT=dg4[:, kk, :], rhs=rawf[:, p0 + kk:p0 + kk + n],
                            start=(kk == 0), stop=(kk == 3)), r=["dg4"] + need, w=[PB(cb_)])
                    S.op("act", lambda e, p0=p0, n=n, dst=dst, g=g, ct=ct, cb_=cb_: e.activation(
                        out=dst[:, p0:p0 + n], in_=bank(cb_)[:, 0:n], func=AF.Silu,
                        bias=cb4[:, g * 4 + ct:g * 4 + ct + 1]), r=[PB(cb_)] + CONST, w=segs(dname + "_p", p0, p0 + n))
                if ct <= 2:
                    srcT = xact if ct < 2 else BT
                    sname = "xact" if ct < 2 else "BT"
                    for tq in range(0, NT, 4):
                        nn = min(4, NT - tq)
                        for j in range(nn):
                            S.op("pe", lambda e, tq=tq, j=j, srcT=srcT: e.transpose(
                                out=bankbf(4)[:, j * 128:(j + 1) * 128], in_=srcT[:, (tq + j) * 128:(tq + j + 1) * 128],
                                identity=ident[:, :]), r=["%s_p%d" % (sname, tq // 4)] + CONST, w=[PB(4)])
                        if ct < 2:
                            S.op("dve", lambda e, tq=tq, nn=nn, ct=ct: e.tensor_copy(
                                out=xtok[:, tq:tq + nn, ct * 128:(ct + 1) * 128],
                                in_=bankbf(4)[:, 0:nn * 128].rearrange("p (j c) -> p j c", j=nn)),
                                r=[PB(4)], w=["xtok"])
                        else:
                            S.op("dve", lambda e, tq=tq, nn=nn: e.tensor_copy(
                                out=Btok[:, tq:tq + nn, :],
                                in_=bankbf(4)[:, 0:nn * 128].rearrange("p (j c) -> p j c", j=nn)),
                                r=[PB(4)], w=["Btok"])
            if dbg and stage == 2 and g == 0:
                dump("xtok", xtok[:, :, :], [128, NT, 256], ["xtok"], BF16)
                dump("Btok", Btok[:, :, :], [128, NT, 128], ["Btok"], BF16)
                dump("CT", CT[:, :], [128, TP], ["CT_p%d" % i_ for i_ in range(5)], BF16)
            S.op(PCE, lambda e, g=g: e.tensor_tensor(
                out=xDall[:, :, :].rearrange("p c (r q) -> p c r q", r=4),
                in0=xtok[:, MT0:NT, :].rearrange("p c (r q) -> p c r q", r=4),
                in1=dsk[:, g * 4:(g + 1) * 4].unsqueeze(1).unsqueeze(3).broadcast_to([128, NMAIN, 4, 64]), op=ALU.mult),
                r=["xtok"] + CONST, w=["xDall"])
            S.op(PCE, lambda e, g=g: e.tensor_tensor(
                out=xtok[:, :, :].rearrange("p c (r q) -> p c r q", r=4),
                in0=xtok[:, :, :].rearrange("p c (r q) -> p c r q", r=4),
                in1=dtA[:, :, g * 4:(g + 1) * 4].unsqueeze(3).broadcast_to([128, NT, 4, 64]), op=ALU.mult),
                r=["xtok", "xDall"] + ["dtA%d" % t for t in range(NT)], w=["xtok"])
            S.op(PCE, lambda e, g=g: e.tensor_tensor(
                out=vall[:, :, :].rearrange("p c (r q) -> p c r q", r=4),
                in0=xtok[:, 0:8, :].rearrange("p c (r q) -> p c r q", r=4),
                in1=dsA[:, 0:8, g * 4:(g + 1) * 4].unsqueeze(3).broadcast_to([128, 8, 4, 64]), op=ALU.mult),
                r=["xtok", "dsA"], w=["vall%d" % c_ for c_ in range(MT0, NT)])
            S.op(PCE, lambda e, g=g: e.tensor_tensor(
                out=xdd[0][:, :, :], in0=xtok[:, 8, :].rearrange("p (r q) -> p r q", r=4),
                in1=dsA[:, 8, g * 4:(g + 1) * 4].unsqueeze(2).broadcast_to([128, 4, 64]), op=ALU.mult),
                r=["xtok", "dsA"], w=["xdd0"])
            for c_ in range(MT0):
                rhs_ = vall[:, c_, :] if c_ < 8 else xdd[0][:, :, :].rearrange("p r q -> p (r q)")
                rn = ("vall%d" % (c_ + MT0)) if c_ < 8 else "xdd0"
                S.op("pe", lambda e, c_=c_, rhs_=rhs_: e.matmul(bank(7)[:, 0:256], lhsT=Btok[:, c_, :], rhs=rhs_,
                                                              start=(c_ == 0), stop=(c_ == MT0 - 1)),
                     r=["Btok", rn], w=[PB(7)])
            S.op("dve", lambda e: e.tensor_copy(out=prev32[:, :, :], in_=bank(7)[:, 0:256].rearrange("p (r q) -> p r q", r=4)),
                 r=[PB(7)], w=["prev32"])
            S.op("act", lambda e: e.activation(out=prevbf[MT0 % 2][:, :, :], in_=prev32[:, :, :], func=AF.Copy),
                 r=["prev32"], w=["prevbf%d" % (MT0 % 2)])

            def stA(c, g=g):
                pa = c % 2
                dAg = dAA[:, c, g * 4:(g + 1) * 4]
                S.op(PCE, lambda e: e.tensor_tensor(
                    out=rhsA[pa][:, :, :], in0=Um[:, :].unsqueeze(1).broadcast_to([128, 4, 128]),
                    in1=dAg.unsqueeze(2).broadcast_to([128, 4, 128]), op=ALU.mult),
                    r=["dAA%d" % c] + CONST, w=["rhsA%d" % pa])

            def stBC(c, g=g):
                pa = c % 2
                S.op("pe", lambda e: e.matmul(bank(pa)[:, :], lhsT=onesf[:, :],
                                              rhs=rhsA[pa][:, :, :].rearrange("p r q -> p (r q)"), start=True, stop=True),
                     r=["rhsA%d" % pa] + CONST, w=[PB(pa)])
                S.op("pe", lambda e: e.matmul(bank(6)[:, pa * 128:(pa + 1) * 128], lhsT=BT[:, c * 128:(c + 1) * 128],
                                              rhs=CT[:, c * 128:(c + 1) * 128], start=True, stop=True),
                     r=["BT_p%d" % i_ for i_ in range(5)] + ["CT_p%d" % i_ for i_ in range(5)], w=[PB(6)])
                S.op("dve", lambda e: e.tensor_tensor(
                    out=segm[pa][:, :, :], in0=bank(pa)[:, :].rearrange("p (r q) -> p r q", r=4),
                    in1=negAcA[:, c, g * 4:(g + 1) * 4].unsqueeze(2).broadcast_to([128, 4, 128]), op=ALU.add),
                    r=[PB(pa), "negAcA"], w=["segm%d" % pa])
                S.op("act", lambda e: e.activation(out=eAb[pa][:, :, :].rearrange("p r q -> p (r q)"), in_=bank(pa)[:, :],
                                                   func=AF.Exp), r=[PB(pa)], w=["eAb%d" % pa])
                S.op("dve", lambda e: e.tensor_tensor(out=CBm[pa][:, :], in0=bank(6)[:, pa * 128:(pa + 1) * 128],
                                                     in1=m01[:, :], op=ALU.mult),
                     r=[PB(6)] + CONST, w=["CBm%d" % pa])
                S.op("act", lambda e: e.activation(out=Lm[pa][:, :, :], in_=segm[pa][:, :, :], func=AF.Exp),
                     r=["segm%d" % pa], w=["Lm%d" % pa])

            def stDE(c, g=g):
                pa = c % 2
                S.op("dve", lambda e: e.scalar_tensor_tensor(
                    out=Mm[pa][:, :, :], in0=Lm[pa][:, :, :], scalar=1.0,
                    in1=CBm[pa][:, :].unsqueeze(1).broadcast_to([128, 4, 128]), op0=ALU.min, op1=ALU.mult),
                    r=["Lm%d" % pa, "CBm%d" % pa], w=["Mm%d" % pa])
                S.op(PCE, lambda e: e.tensor_tensor(
                    out=CTs[pa][:, :, :], in0=eAb[pa][:, :, :],
                    in1=CT[:, c * 128:(c + 1) * 128].unsqueeze(1).broadcast_to([128, 4, 128]), op=ALU.mult),
                    r=["eAb%d" % pa] + ["CT_p%d" % i_ for i_ in range(5)], w=["CTs%d" % pa])
                for k in range(KT):
                    S.op("pe", lambda e, k=k: e.matmul(bank(2 + pa)[:, 0:256], lhsT=lhs_tok(k, c), rhs=wz[:, k, :],
                                                       start=(k == 0), stop=(k == KT - 1)),
                         r=["wsl2", "hnT%d" % c], w=[PB(2 + pa)])
                S.op("act", lambda e: e.activation(out=szz[pa][:, :], in_=bank(2 + pa)[:, 0:256], func=AF.Tanh, scale=0.5),
                     r=[PB(2 + pa)], w=["szz%d" % pa])
                S.op("dve", lambda e: e.scalar_tensor_tensor(out=szz[pa][:, :], in0=szz[pa][:, :], scalar=1.0,
                                                            in1=bank(2 + pa)[:, 0:256], op0=ALU.add, op1=ALU.mult),
                     r=["szz%d" % pa, PB(2 + pa)], w=["szz%d" % pa])

            def stS(c, g=g):
                pa = c % 2
                S.op(PCE, lambda e: e.tensor_tensor(
                    out=xdd[pa][:, :, :], in0=xtok[:, c, :].rearrange("p (r q) -> p r q", r=4),
                    in1=dsA[:, c, g * 4:(g + 1) * 4].unsqueeze(2).broadcast_to([128, 4, 64]), op=ALU.mult),
                    r=["xtok", "dsA"], w=["xdd%d" % pa])
                S.op("pe", lambda e: e.matmul(bank(7)[:, pa * 256:(pa + 1) * 256], lhsT=Btok[:, c, :],
                                              rhs=xdd[pa][:, :, :].rearrange("p r q -> p (r q)"),
                                              start=True, stop=True), r=["Btok", "xdd%d" % pa], w=[PB(7)])

            def stF(c, g=g):
                pa = c % 2
                main = c >= MT0
                if main:
                    yb = bank(4 + pa)
                    S.op("pe", lambda e: e.matmul(yb[:, 0:256], lhsT=ident[:, :], rhs=xDall[:, c - MT0, :],
                                                  start=True, stop=False), r=["xDall"] + CONST, w=[PB(4 + pa)])
                    for r_ in range(4):
                        S.op("pe", lambda e, r_=r_: e.matmul(yb[:, r_ * 64:(r_ + 1) * 64], lhsT=Mm[pa][:, r_, :],
                                                            rhs=xtok[:, c, r_ * 64:(r_ + 1) * 64], start=False, stop=False),
                             r=["Mm%d" % pa, "xtok"], w=[PB(4 + pa)])
                    for r_ in range(4):
                        S.op("pe", lambda e, r_=r_: e.matmul(yb[:, r_ * 64:(r_ + 1) * 64], lhsT=CTs[pa][:, r_, :],
                                                            rhs=prevbf[pa][:, r_, :], start=False, stop=(r_ == 3)),
                             r=["CTs%d" % pa, "prevbf%d" % pa], w=[PB(4 + pa)])
                if c < NT - 1:
                    S.op("dve", lambda e: e.tensor_tensor(
                        out=prev32[:, :, :], in0=prev32[:, :, :],
                        in1=cdA[:, c, g * 4:(g + 1) * 4].unsqueeze(2).broadcast_to([128, 4, 64]), op=ALU.mult),
                        r=["prev32", "cdA"], w=["prev32"])
                    S.op("dve", lambda e: e.tensor_tensor(
                        out=prev32[:, :, :], in0=prev32[:, :, :],
                        in1=bank(7)[:, pa * 256:(pa + 1) * 256].rearrange("p (r q) -> p r q", r=4), op=ALU.add),
                        r=["prev32", PB(7)], w=["prev32"])
                    S.op("act", lambda e: e.activation(out=prevbf[1 - pa][:, :, :], in_=prev32[:, :, :], func=AF.Copy),
                         r=["prev32"], w=["prevbf%d" % (1 - pa)])

            def stV(c, g=g):
                pa = c % 2
                yb = bank(4 + pa)
                S.op("dve", lambda e: e.scalar_tensor_tensor(out=vall[:, c - MT0, :], in0=yb[:, 0:256], scalar=0.5,
                                                            in1=szz[pa][:, :], op0=ALU.mult, op1=ALU.mult),
                     r=[PB(4 + pa), "szz%d" % pa], w=["vall%d" % c])
                S.op("act", lambda e: e.activation(out=junk2[:, :], in_=vall[:, c - MT0, :], func=AF.Square,
                                                   accum_out=gsA[:, c - MT0:c - MT0 + 1]),
                     r=["vall%d" % c], w=["junk2", "gsA%d" % c])

            def ismain(c):
                return MT0 <= c < NT

            for step in range(MT0 - 3, NT + 1):
                if ismain(step + 3):
                    stA(step + 3)
                if ismain(step + 2):
                    stBC(step + 2)
                if ismain(step - 1):
                    stV(step - 1)
                if ismain(step + 1):
                    stDE(step + 1)
                if MT0 <= step + 1 < NT - 1:
                    stS(step + 1)
                if MT0 <= step < NT:
                    stF(step)
            S.op("dve", lambda e: e.tensor_scalar(out=gsA[:, 8:16], in0=gsA[:, 0:8], scalar1=1.0 / 256.0, scalar2=EPS,
                                                 op0=ALU.mult, op1=ALU.add),
                 r=["gsA%d" % c for c in range(MT0, NT)], w=["gsB"])
            S.op("act", lambda e: e.activation(out=gsA[:, 16:24], in_=gsA[:, 8:16], func=AF.Sqrt), r=["gsB"], w=["gsC"])
            S.op("dve", lambda e: e.reciprocal(out=gsA[:, 24:32], in_=gsA[:, 16:24]), r=["gsC"], w=["gsD"])
            for c in range(MT0, NT):
                q4 = (c - MT0) % 4
                tb = (4, 5, 2, 3)[q4]
                S.op("dve", lambda e, c=c, q4=q4: e.scalar_tensor_tensor(
                    out=vn4[q4][:, :], in0=vall[:, c - MT0, :], scalar=gsA[:, 24 + c - MT0:25 + c - MT0],
                    in1=nwg[:, :], op0=ALU.mult, op1=ALU.mult), r=["vall%d" % c, "gsD", "nwg"], w=["vn%d" % q4])
                for i in range(2):
                    S.op("pe", lambda e, i=i, q4=q4, tb=tb: e.transpose(
                        out=bankbf(tb)[:, 512 + i * 128:512 + (i + 1) * 128], in_=vn4[q4][:, i * 128:(i + 1) * 128],
                        identity=ident[:, :]), r=["vn%d" % q4] + CONST, w=[PB(tb)])
                S.op("act", lambda e, c=c, g=g, tb=tb: e.activation(
                    out=yT[:, 16 + 2 * g:18 + 2 * g, (c - MT0) * 128:(c - MT0 + 1) * 128],
                    in_=bankbf(tb)[:, 512:768].rearrange("p (i c) -> p i c", i=2), func=AF.Copy),
                    r=[PB(tb)], w=["yTs"])
        if dbg and stage == 2:
            dump("yTs", yT[:, 16:32, :], [128, 16, 1024], ["yTs"], BF16)

    if stage >= 3:
        for j_ in range(2):
            wload(j_, wcv_d[j_], wsl[j_][:, :].rearrange("p (k c) -> p k c", k=KT), key=("cv", 0, j_))
        S.barrier()
        o = T0
        convout, _, n_ = sb("convout", [128, 16, 512], F32, at=o); o += n_
        ubuf, _, n_ = sb("ubuf", [128, 544], BF16, at=o); o += n_
        sgm, _, n_ = sb("sgm", [128, 512], F32, at=o); o += n_
        sqb, _, n_ = sb("sqb", [128, 512], BF16, at=o); o += n_
        cbf, _, n_ = sb("cbf", [128, 512], BF16, at=o); o += n_
        assert o <= T0 + 40960, o
        st_mean, _, n_ = sb("st_mean", [128, 512], F32, at=o); o += n_
        ubuf2 = nc.alloc_sbuf_tensor_at("ubuf2", [128, 544], BF16, offset=HN0_OFF[0])
        sgm2 = nc.alloc_sbuf_tensor_at("sgm2", [128, 512], F32, offset=HN0_OFF[0] + 1088)
        ubufs = [ubuf, ubuf2]
        sgms = [sgm, sgm2]
        assert o <= T0 + 40960, o
        dg31 = nc.alloc_sbuf_tensor_at("dg31", [128, 31, 128], BF16, offset=CEND_W2[0])
        hm0 = MT0 * 128 - PAD
        for mb in range(2):
            m0 = hm0 + mb * 512
            def glu(j, m0=m0, mb=mb):
                sl_ = j % 2
                wv = wsl[sl_][:, :].rearrange("p (k c) -> p k c", k=KT)
                wload(sl_, wcv_d[j], wv, key=("cv", mb, j))
                b0, b1 = (0, 1) if j % 2 == 0 else (6, 7)
                ub = ubufs[j % 2]
                sg = sgms[j % 2]
                pj = j % 2
                specs = ((m0 - 32, 32, 0, bank(5)[:, pj * 64:pj * 64 + 32], bank(5)[:, pj * 64 + 32:pj * 64 + 64], 5, 5),
                         (m0, 512, 32, bank(b0)[:, :], bank(b1)[:, :], b0, b1))
                for (t0, n, uo, pv, pg, nb0, nb1) in specs:
                    for half, pp_, nb in ((0, pv, nb0), (1, pg, nb1)):
                        for k in range(KT):
                            S.op("pe", lambda e, k=k, t0=t0, n=n, half=half, pp_=pp_: e.matmul(
                                pp_, lhsT=wv[:, k, half * 128:(half + 1) * 128],
                                rhs=hnT[:, k, t0:t0 + n], start=(k == 0), stop=(k == KT - 1)),
                                r=["wsl%d" % sl_] + hn_res(MT0 - 1, NT), w=[PB(nb)])
                for (t0, n, uo, pv, pg, nb0, nb1) in specs:
                    S.op("act", lambda e, n=n, pg=pg: e.activation(out=sg[:, 0:n], in_=pg, func=AF.Sigmoid),
                         r=[PB(nb1)], w=["sgm%d" % (j % 2)])
                    S.op("dve", lambda e, n=n, uo=uo, pv=pv: e.tensor_tensor(out=ub[:, uo:uo + n], in0=pv,
                                                                            in1=sg[:, 0:n], op=ALU.mult),
                         r=[PB(nb0), "sgm%d" % (j % 2)], w=["ubuf%d" % (j % 2)])

            def dbuild(j):
                for kk in range(31):
                    col = cw31[:, j * 31 + kk:j * 31 + kk + 1]
                    if kk < 16:
                        S.op("act", lambda e, kk=kk, col=col: e.activation(out=dg31[:, kk, :], in_=ident[:, :],
                                                                          func=AF.Copy, scale=col),
                             r=CONST, w=["dg31a"])
                    else:
                        S.op("dve", lambda e, kk=kk, col=col: e.tensor_scalar(
                            out=dg31[:, kk, :], in0=ident[:, :], scalar1=col, scalar2=None, op0=ALU.mult),
                            r=CONST, w=["dg31b"])

            def conv(j):
                ub = ubufs[j % 2]
                for kk in range(31):
                    S.op("pe", lambda e, kk=kk: e.matmul(bank(2)[:, :], lhsT=dg31[:, kk, :],
                                                        rhs=ub[:, 2 + kk:2 + kk + 512], start=(kk == 0),
                                                        stop=(kk == 30)),
                         r=["dg31a" if kk < 16 else "dg31b", "ubuf%d" % (j % 2)], w=[PB(2)])
                S.op("act", lambda e: e.activation(out=convout[:, j, :], in_=bank(2)[:, :], func=AF.Identity,
                                                   bias=cb31[:, j:j + 1]), r=[PB(2)] + CONST, w=["convout%d" % j])
                S.op("act", lambda e: e.activation(out=sqb[:, :], in_=bank(2)[:, :], func=AF.Square,
                                                   bias=cb31[:, j:j + 1]), r=[PB(2)] + CONST, w=["sqb"])
                S.op("dve", lambda e: e.tensor_copy(out=cbf[:, :], in_=convout[:, j, :]),
                     r=["convout%d" % j], w=["cbf"])
                S.op("pe", lambda e: e.matmul(bank(3)[:, :], lhsT=onesb[:, :], rhs=cbf[:, :], start=(j == 0),
                                              stop=(j == 15)), r=["cbf"] + CONST, w=[PB(3)])
                S.op("pe", lambda e: e.matmul(bank(4)[:, :], lhsT=onesb[:, :], rhs=sqb[:, :], start=(j == 0),
                                              stop=(j == 15)), r=["sqb"] + CONST, w=[PB(4)])

            dbuild(0)
            glu(0)
            for j in range(16):
                if j + 1 < 16:
                    glu(j + 1)
                conv(j)
                if j + 1 < 16:
                    dbuild(j + 1)
            S.op("dve", lambda e: e.tensor_scalar(out=st_mean[:, :], in0=bank(3)[:, :], scalar1=1.0 / 2048.0,
                                                 scalar2=None, op0=ALU.mult), r=[PB(3)], w=["st_mean"])
            S.op("dve", lambda e: e.tensor_tensor(out=sgm[:, :], in0=st_mean[:, :], in1=st_mean[:, :], op=ALU.mult),
                 r=["st_mean"], w=["sgm0"])
            S.op("dve", lambda e: e.scalar_tensor_tensor(out=sgm[:, :], in0=bank(4)[:, :], scalar=1.0 / 2048.0,
                                                        in1=sgm[:, :], op0=ALU.mult, op1=ALU.subtract),
                 r=[PB(4), "sgm0"], w=["sgm0"])
            S.op("dve", lambda e: e.tensor_scalar(out=sgm[:, :], in0=sgm[:, :], scalar1=EPS, scalar2=None,
                                                 op0=ALU.add), r=["sgm0"], w=["sgm0"])
            S.op("act", lambda e: e.activation(out=sgm[:, :], in_=sgm[:, :], func=AF.Sqrt), r=["sgm0"], w=["sgm0"])
            S.op("dve", lambda e: e.reciprocal(out=sgm[:, :], in_=sgm[:, :]), r=["sgm0"], w=["sgm0"])
            for j in range(16):
                S.op("dve", lambda e, j=j: e.tensor_tensor(out=convout[:, j, :], in0=convout[:, j, :],
                                                          in1=st_mean[:, :], op=ALU.subtract),
                     r=["convout%d" % j, "st_mean"], w=["convout%d" % j])
                S.op("dve", lambda e, j=j: e.tensor_tensor(out=convout[:, j, :], in0=convout[:, j, :], in1=sgm[:, :],
                                                          op=ALU.mult), r=["convout%d" % j, "sgm0"], w=["convout%d" % j])
                S.op("act", lambda e, j=j, mb=mb: e.activation(out=yT[:, j, mb * 512:(mb + 1) * 512],
                                                              in_=convout[:, j, :], func=AF.Silu,
                                                              bias=lnb[:, j:j + 1], scale=lng[:, j:j + 1]),
                     r=["convout%d" % j] + CONST, w=["yTc%d" % j])
        for j_ in range(2):
            wload(j_, wcs_d[j_], wsl[j_][:, 0:2048].rearrange("p (k c) -> p k c", k=KT), key=("cs", j_))
        S.barrier()
        gate = [sgm, sgm2]
        for j in range(16):
            sl_ = j % 3
            wv = wsl[sl_][:, 0:2048].rearrange("p (k c) -> p k c", k=KT)
            wload(sl_, wcs_d[j], wv, key=("cs", j))
            for mb in range(2):
                bb = (5, 0)[mb]
                for k in range(KT):
                    S.op("pe", lambda e, k=k, wv=wv, mb=mb, bb=bb: e.matmul(
                        bank(bb)[:, :], lhsT=wv[:, k, :], rhs=hnT[:, k, hm0 + mb * 512:hm0 + (mb + 1) * 512],
                        start=(k == 0), stop=(k == KT - 1)), r=["wsl%d" % sl_] + hn_res(MT0, NT), w=[PB(bb)])
                S.op("act", lambda e, mb=mb, bb=bb: e.activation(out=gate[mb][:, :], in_=bank(bb)[:, :], func=AF.Silu),
                     r=[PB(bb)], w=["sgm%d" % mb])
                S.op("dve", lambda e, j=j, mb=mb: e.tensor_tensor(out=yT[:, j, mb * 512:(mb + 1) * 512],
                                                                 in0=yT[:, j, mb * 512:(mb + 1) * 512],
                                                                 in1=gate[mb][:, :], op=ALU.mult),
                     r=["yTc%d" % j, "sgm%d" % mb], w=["yTc%d" % j])
        if dbg and stage == 3:
            dump("yTc", yT[:, 0:16, :], [128, 16, 1024], ["yTc%d" % j for j in range(16)], BF16)

    if stage >= 4:
        for c_ in range(3):
            wload(c_, wout_d[c_], wsl[c_][:, :].rearrange("p (k c) -> p k c", k=32), key=("wo", c_))
        S.barrier()
        hout = nc.alloc_sbuf_tensor_at("hout", [128, NMAIN, D], F32, offset=BASE)
        fnw, _, _ = sb("fnwS", [128, D], F32, at=T0)
        junk3, _, _ = sb("junk3", [128, D], BF16, at=T0 + 8192)
        osml, _, _ = sb("osml", [128, 32], F32, at=T0 + 12288)
        S.op("sp", lambda e: e.dma_start(out=fnw[:, :], in_=fnw_d), w=["fnw"], dma="fnw")
        for t in range(NMAIN):
            S.op("sp", lambda e, t=t: e.dma_start(out=hout[:, t, :],
                                                  in_=xin[(MT0 + t) * 128:(MT0 + t + 1) * 128, :]),
                 w=["hout%d" % t], dma="hres%d" % t)
        for cbk in range(16):
            sl_ = cbk % 3
            wv = wsl[sl_][:, :].rearrange("p (k c) -> p k c", k=32)
            wload(sl_, wout_d[cbk], wv, key=("wo", cbk))
            for t in range(NMAIN):
                pb = (cbk * NMAIN + t) % 4
                for k in range(32):
                    S.op("pe", lambda e, k=k, t=t, pb=pb, wv=wv: e.matmul(
                        bank(pb)[:, 0:128], lhsT=yT[:, k, t * 128:(t + 1) * 128], rhs=wv[:, k, :],
                        start=(k == 0), stop=(k == 31)), r=["wsl%d" % sl_, "yTs"] + ["yTc%d" % j for j in range(16)], w=[PB(pb)])
                S.op("dve", lambda e, t=t, pb=pb, cbk=cbk: e.tensor_tensor(
                    out=hout[:, t, cbk * 128:(cbk + 1) * 128], in0=bank(pb)[:, 0:128],
                    in1=hout[:, t, cbk * 128:(cbk + 1) * 128], op=ALU.add), r=[PB(pb), "hout%d" % t], w=["hout%d" % t])
        for t in range(NMAIN):
            S.op("act", lambda e, t=t: e.activation(out=junk3[:, :], in_=hout[:, t, :], func=AF.Square,
                                                   accum_out=osml[:, 4 * t:4 * t + 1]),
                 r=["hout%d" % t], w=["junk3", "oA%d" % t])
            S.op("dve", lambda e, t=t: e.tensor_scalar(out=osml[:, 4 * t + 1:4 * t + 2], in0=osml[:, 4 * t:4 * t + 1],
                                                      scalar1=1.0 / D, scalar2=EPS, op0=ALU.mult, op1=ALU.add),
                 r=["oA%d" % t], w=["oB%d" % t])
            S.op("act", lambda e, t=t: e.activation(out=osml[:, 4 * t + 2:4 * t + 3], in_=osml[:, 4 * t + 1:4 * t + 2],
                                                   func=AF.Sqrt), r=["oB%d" % t], w=["oC%d" % t])
            S.op("dve", lambda e, t=t: e.reciprocal(out=osml[:, 4 * t + 3:4 * t + 4], in_=osml[:, 4 * t + 2:4 * t + 3]),
                 r=["oC%d" % t], w=["oD%d" % t])
            S.op("dve", lambda e, t=t: e.scalar_tensor_tensor(out=hout[:, t, :], in0=hout[:, t, :],
                                                             scalar=osml[:, 4 * t + 3:4 * t + 4], in1=fnw[:, :],
                                                             op0=ALU.mult, op1=ALU.mult),
                 r=["hout%d" % t, "oD%d" % t, "fnw"], w=["hout%d" % t])
            S.op("sp", lambda e, t=t: e.dma_start(out=out_d[t * 128:(t + 1) * 128, :], in_=hout[:, t, :]),
                 r=["hout%d" % t], w=["outd%d" % t], dma="ost")
    S.barrier()
    S.emit(nc)
    return nc, dbg_outs


def prep_inputs(inp):
    f32 = np.float32
    x = np.asarray(inp["x"], f32)
    meta = np.asarray(inp["meta_tokens"], f32)
    w_in = np.asarray(inp["w_in"], f32)[0]
    C = 2048

    def pk(v, n):
        return np.ascontiguousarray(np.asarray(v, f32).reshape(n, 128).T)

    def bc(v):
        return np.ascontiguousarray(np.broadcast_to(np.asarray(v, f32).reshape(1, -1), (128, np.asarray(v).size)))

    shared = {}
    shared["normw"] = pk(inp["norm_w"][0], 16)
    def blk(wcols):
        c = wcols.shape[1]
        return np.ascontiguousarray(wcols.reshape(16, 128, c).transpose(1, 0, 2))

    shared["w_dt"] = blk(w_in[:, 12288:12320])
    zc, xc, Bc, Cc = 3 * C, 4 * C, 5 * C, 5 * C + 1024
    wssd = np.empty((8, 3, 128, 16, 256), f32)
    for g in range(8):
        wssd[g, 0] = blk(w_in[:, xc + g * 256: xc + (g + 1) * 256])
        wssd[g, 1, :, :, 0:128] = blk(w_in[:, Bc + g * 128: Bc + (g + 1) * 128])
        wssd[g, 1, :, :, 128:256] = blk(w_in[:, Cc + g * 128: Cc + (g + 1) * 128])
        wssd[g, 2] = blk(w_in[:, zc + g * 256: zc + (g + 1) * 256])
    shared["w_ssd"] = wssd
    wcv = np.empty((16, 128, 16, 256), f32)
    wcs = np.empty((16, 128, 16, 128), f32)
    for j in range(16):
        wcv[j, :, :, 0:128] = blk(w_in[:, j * 128:(j + 1) * 128])
        wcv[j, :, :, 128:256] = blk(w_in[:, C + j * 128: C + (j + 1) * 128])
        wcs[j] = blk(w_in[:, 2 * C + j * 128: 2 * C + (j + 1) * 128])
    shared["w_cv"] = wcv
    shared["w_cs"] = wcs
    wo = np.asarray(inp["w_out"], f32)[0]
    shared["w_out"] = np.ascontiguousarray(wo.reshape(32, 128, 16, 128).transpose(2, 1, 0, 3))
    scw = np.asarray(inp["ssd_conv_w"], f32)[0]
    scb = np.asarray(inp["ssd_conv_b"], f32)[0]
    cw4 = np.empty((128, 8, 4, 4), f32)
    cb4 = np.empty((128, 8, 4), f32)
    for g in range(8):
        chs = [g * 256, g * 256 + 128, 2048 + g * 128, 3072 + g * 128]
        for ct in range(4):
            cw4[:, g, ct, :] = scw[:, chs[ct]:chs[ct] + 128].T
            cb4[:, g, ct] = scb[chs[ct]:chs[ct] + 128]
    shared["cw4"] = cw4.reshape(128, 128)
    shared["cb4"] = cb4.reshape(128, 32)
    c31 = np.asarray(inp["conf_dw_w"], f32)[0]
    shared["cw31"] = np.ascontiguousarray(c31.reshape(31, 16, 128).transpose(2, 1, 0)).reshape(128, 16 * 31)
    shared["cb31"] = pk(inp["conf_dw_b"][0], 16)
    shared["lng"] = pk(inp["conf_ln_g"][0], 16)
    shared["lnb"] = pk(inp["conf_ln_b"][0], 16)
    shared["dtb"] = bc(inp["dt_bias"][0])
    shared["alog"] = bc(inp["A_log"][0])
    shared["dsk"] = bc(inp["D_skip"][0])
    shared["snw"] = bc(inp["ssd_norm_w"][0])
    shared["fnw"] = bc(inp["final_norm_w"])
    bf = ml_dtypes.bfloat16
    shared["ident"] = np.eye(128, dtype=f32).astype(bf)
    shared["onesb"] = np.ones((128, 128), f32).astype(bf)
    shared["onesf"] = np.ones((128, 128), f32)
    shared["U"] = np.triu(np.ones((128, 128), f32))
    shared["m01"] = np.triu(np.ones((128, 128), f32)).astype(bf)
    in_maps = []
    for core in range(8):
        b, half = core // 2, core % 2
        xi = np.zeros((TP, D), f32)
        mk = np.zeros((TP,), f32)
        if half == 0:
            xi[1136:1152] = meta
            xi[1152:2176] = x[b, 0:1024]
            mk[1136:] = 1.0
        else:
            xi[112:128] = meta
            xi[128:2176] = x[b, 0:2048]
            mk[112:] = 1.0
        m = dict(shared)
        m["xin"] = xi
        m["mask"] = np.ascontiguousarray(mk.reshape(NT, 128).T)
        in_maps.append(m)
    return in_maps


_NC_CACHE = {}


def kernel(**inputs):
    in_maps = prep_inputs(inputs)
    if "nc" not in _NC_CACHE:
        _NC_CACHE["nc"] = build()[0]
    nc = _NC_CACHE["nc"]
    res = run_bass_kernel_spmd(nc, in_maps, core_ids=list(range(8)))
    out = np.empty((4, 2048, D), np.float32)
    for core in range(8):
        b, half = core // 2, core % 2
        out[b, half * 1024:(half + 1) * 1024] = np.asarray(res.results[core]["out"], np.float32)
    return out
```
